# Optimizing a Trainium2 kernel written in Bass

```python
import jax
import jax.numpy as jnp
from jax import lax
import numpy as np

D_MODEL = 1024
BATCH = 8
SEQ = 2048
DEPTH = 2

GRID_W = 64
CTX_LEN = 256
CONV_DIM = D_MODEL
CONV_WIDTH = 31
MLSTM_HEADS = 4
MLSTM_DIM = 2 * D_MODEL
MLSTM_HEAD_DIM = MLSTM_DIM // MLSTM_HEADS
QK_CONV_WIDTH = 5
CHUNK = 128
D_FF = 4 * D_MODEL
N_MOD = 6
EPS = 1e-6
M_INIT = -1e30

IN_SIZES = (CONV_DIM, CONV_DIM, MLSTM_DIM, MLSTM_DIM, MLSTM_DIM, MLSTM_DIM, 4 * MLSTM_HEADS, D_MODEL, D_MODEL)
IN_DIM = sum(IN_SIZES)
IN_SPLITS = tuple(sum(IN_SIZES[:i + 1]) for i in range(len(IN_SIZES) - 1))

kernel_name = 'hybrid_conformer_mlstm_prefix_dit'


def _rmsnorm(x, g):
    xf = x.astype(jnp.float32)
    y = xf * lax.rsqrt(jnp.mean(xf * xf, axis=-1, keepdims=True) + EPS)
    return (y * g.astype(jnp.float32)).astype(x.dtype)


def _layernorm(x, g, b):
    xf = x.astype(jnp.float32)
    mu = jnp.mean(xf, axis=-1, keepdims=True)
    var = jnp.mean(jnp.square(xf - mu), axis=-1, keepdims=True)
    y = (xf - mu) * lax.rsqrt(var + EPS)
    return (y * g.astype(jnp.float32) + b.astype(jnp.float32)).astype(x.dtype)


def _dwconv(x, w):
    k = w.shape[0]
    return lax.conv_general_dilated(x, w[:, None, :].astype(x.dtype), window_strides=(1,),
                                    padding=[(k // 2, k // 2)],
                                    dimension_numbers=('NWC', 'WIO', 'NWC'),
                                    feature_group_count=x.shape[-1])


def _modulation(cvec, w_mod, b_mod):
    m = jax.nn.silu(cvec) @ w_mod + b_mod
    if m.ndim == 2:
        m = m[:, None, :]
    return jnp.split(m, N_MOD, axis=-1)


def _modulate(h, shift, scale):
    return h * (1 + scale) + shift


def _init_state(bsz):
    h, d = MLSTM_HEADS, MLSTM_HEAD_DIM
    return (jnp.zeros((bsz, h, d, d), jnp.float32),
            jnp.zeros((bsz, h, d), jnp.float32),
            jnp.full((bsz, h), M_INIT, jnp.float32))


def _mlstm_scan(q, k, v, ig, fg, state):
    bsz, t, nh, dh = q.shape
    nc = t // CHUNK

    def to_chunks(a):
        a = a.reshape((bsz, nc, CHUNK, nh) + a.shape[3:])
        return jnp.moveaxis(a, (1, 3), (0, 2))

    logf = jax.nn.log_sigmoid(fg)
    xs = (to_chunks(q), to_chunks(k), to_chunks(v), to_chunks(ig), to_chunks(logf))
    causal = jnp.tril(jnp.ones((CHUNK, CHUNK), bool))

    def step(carry, inp):
        c_st, n_st, m_st = carry
        qc, kc, vc, ic, lfc = inp
        b = jnp.cumsum(lfc, axis=-1)
        log_d = jnp.where(causal, b[..., :, None] - b[..., None, :] + ic[..., None, :], -jnp.inf)
        inter = b + m_st[..., None]
        m_t = jnp.maximum(inter, jnp.max(log_d, axis=-1))
        w_intra = jnp.exp(log_d - m_t[..., None])
        w_inter = jnp.exp(inter - m_t)
        s = jnp.einsum('bhtd,bhsd->bhts', qc, kc) * w_intra
        num = (w_inter[..., None] * jnp.einsum('bhtd,bhde->bhte', qc, c_st)
               + jnp.einsum('bhts,bhse->bhte', s, vc))
        den = w_inter * jnp.einsum('bhtd,bhd->bht', qc, n_st) + jnp.sum(s, axis=-1)
        h = num / jnp.maximum(jnp.abs(den), jnp.exp(-m_t))[..., None]
        g = b[..., -1:] - b + ic
        inter_end = b[..., -1] + m_st
        m_new = jnp.maximum(inter_end, jnp.max(g, axis=-1))
        ws = jnp.exp(g - m_new[..., None])
        we = jnp.exp(inter_end - m_new)
        kw = kc * ws[..., None]
        c_new = we[..., None, None] * c_st + jnp.einsum('bhsd,bhse->bhde', kw, vc)
        n_new = we[..., None] * n_st + jnp.sum(kw, axis=2)
        return (c_new, n_new, m_new), h

    state, hs = lax.scan(step, state, xs)
    h = jnp.moveaxis(hs, (0, 2), (1, 3)).reshape(bsz, t, nh, dh)
    return h, state


def _bi_scan(q, k, v, i_f, f_f, i_b, f_b, st_f, st_b):
    h_f, st_f = _mlstm_scan(q, k, v, i_f, f_f, st_f)
    fl = lambda a: jnp.flip(a, axis=1)
    h_b, st_b = _mlstm_scan(fl(q), fl(k), fl(v), fl(i_b), fl(f_b), st_b)
    return h_f + fl(h_b), st_f, st_b


def _mlstm_inputs(p, qk_w, gate_b):
    q = jax.nn.silu(_dwconv(p[2], qk_w[:, :MLSTM_DIM]))
    k = jax.nn.silu(_dwconv(p[3], qk_w[:, MLSTM_DIM:])) * (MLSTM_HEAD_DIM ** -0.5)
    bsz, t, _ = q.shape
    shp = (bsz, t, MLSTM_HEADS, MLSTM_HEAD_DIM)
    g = (p[6] + gate_b).astype(jnp.float32).reshape(bsz, t, 4, MLSTM_HEADS)
    return (q.astype(jnp.float32).reshape(shp), k.astype(jnp.float32).reshape(shp),
            p[4].astype(jnp.float32).reshape(shp),
            g[:, :, 0], g[:, :, 1], g[:, :, 2], g[:, :, 3])


def _mixer_out(p, h, conv_rows, dw_w, ln_g, ln_b, w_conv_out, m_norm_g, w_m_out, w_out):
    u = p[0] * jax.nn.sigmoid(p[1])
    bsz, t, cdim = u.shape
    if conv_rows is None:
        u = _dwconv(u, dw_w)
    else:
        u = _dwconv(u.reshape(bsz * conv_rows, GRID_W, cdim), dw_w).reshape(bsz, t, cdim)
    y_conv = jax.nn.silu(_layernorm(u, ln_g, ln_b)) @ w_conv_out
    hn = h * lax.rsqrt(jnp.mean(h * h, axis=-1, keepdims=True) + EPS)
    hn = hn * m_norm_g.astype(jnp.float32).reshape(MLSTM_HEADS, MLSTM_HEAD_DIM)
    hm = hn.reshape(bsz, t, MLSTM_DIM).astype(p[5].dtype) * jax.nn.sigmoid(p[5])
    y_m = hm @ w_m_out
    y = jax.nn.sigmoid(p[7]) * y_conv + jax.nn.sigmoid(p[8]) * y_m
    return y @ w_out


def _ffn(h, w1, w2):
    return jnp.square(jax.nn.relu(h @ w1)) @ w2


def setup_inputs(seed: int = 0) -> dict:
    key = jax.random.key(seed)
    ks = iter(jax.random.split(key, 32))
    nrm = lambda shape, s: jax.random.normal(next(ks), shape, jnp.float32) * s
    gain = lambda shape: 1.0 + nrm(shape, 0.02)
    i_bias = nrm((DEPTH, 2, MLSTM_HEADS), 0.1)
    f_bias = jnp.linspace(3.0, 6.0, MLSTM_HEADS, dtype=jnp.float32) + nrm((DEPTH, 2, MLSTM_HEADS), 0.1)
    gate_b = jnp.stack([i_bias[:, 0], f_bias[:, 0], i_bias[:, 1], f_bias[:, 1]], axis=1).reshape(DEPTH, 4 * MLSTM_HEADS)
    return {
        'x': nrm((BATCH, SEQ, D_MODEL), 1.0),
        'c': nrm((BATCH, D_MODEL), 1.0),
        'ctx': nrm((BATCH, CTX_LEN, D_MODEL), 1.0),
        'c_ctx': nrm((D_MODEL,), 1.0),
        'w_mod': nrm((DEPTH, D_MODEL, N_MOD * D_MODEL), 0.5 * D_MODEL ** -0.5),
        'b_mod': nrm((DEPTH, N_MOD * D_MODEL), 0.02),
        'norm1_g': gain((DEPTH, D_MODEL)),
        'w_in': nrm((DEPTH, D_MODEL, IN_DIM), D_MODEL ** -0.5),
        'mlstm_gate_b': gate_b,
        'qk_conv_w': nrm((DEPTH, QK_CONV_WIDTH, 2 * MLSTM_DIM), QK_CONV_WIDTH ** -0.5),
        'conv_dw_w': nrm((DEPTH, CONV_WIDTH, CONV_DIM), CONV_WIDTH ** -0.5),
        'conv_ln_g': gain((DEPTH, CONV_DIM)),
        'conv_ln_b': nrm((DEPTH, CONV_DIM), 0.02),
        'w_conv_out': nrm((DEPTH, CONV_DIM, D_MODEL), CONV_DIM ** -0.5),
        'mlstm_norm_g': gain((DEPTH, MLSTM_DIM)),
        'w_mlstm_out': nrm((DEPTH, MLSTM_DIM, D_MODEL), MLSTM_DIM ** -0.5),
        'w_out': nrm((DEPTH, D_MODEL, D_MODEL), D_MODEL ** -0.5),
        'norm2_g': gain((DEPTH, D_MODEL)),
        'w_ff1': nrm((DEPTH, D_MODEL, D_FF), D_MODEL ** -0.5),
        'w_ff2': nrm((DEPTH, D_FF, D_MODEL), D_FF ** -0.5),
        'final_g': gain((D_MODEL,)),
    }


def reference(x, c, ctx, c_ctx, w_mod, b_mod, norm1_g, w_in, mlstm_gate_b, qk_conv_w, conv_dw_w,
              conv_ln_g, conv_ln_b, w_conv_out, mlstm_norm_g, w_mlstm_out, w_out, norm2_g,
              w_ff1, w_ff2, final_g):
    bsz = x.shape[0]
    rows = x.shape[1] // GRID_W
    xl, xc = x, ctx
    for l in range(DEPTH):
        last = l == DEPTH - 1
        sh1_l, sc1_l, ga1_l, sh2_l, sc2_l, ga2_l = _modulation(c, w_mod[l], b_mod[l])
        sh1_c, sc1_c, ga1_c, sh2_c, sc2_c, ga2_c = _modulation(c_ctx, w_mod[l], b_mod[l])
        hl = _modulate(_rmsnorm(xl, norm1_g[l]), sh1_l, sc1_l)
        hc = _modulate(_rmsnorm(xc, norm1_g[l]), sh1_c, sc1_c)
        pl = jnp.split(hl @ w_in[l], IN_SPLITS, axis=-1)
        pc = jnp.split(hc @ w_in[l], IN_SPLITS, axis=-1)
        init = _init_state(bsz)
        h_c, st_f, st_b = _bi_scan(*_mlstm_inputs(pc, qk_conv_w[l], mlstm_gate_b[l]), init, init)
        h_l, _, _ = _bi_scan(*_mlstm_inputs(pl, qk_conv_w[l], mlstm_gate_b[l]), st_f, st_b)
        yl = _mixer_out(pl, h_l, rows, conv_dw_w[l], conv_ln_g[l], conv_ln_b[l], w_conv_out[l],
                        mlstm_norm_g[l], w_mlstm_out[l], w_out[l])
        xl = xl + ga1_l * yl
        xl = xl + ga2_l * _ffn(_modulate(_rmsnorm(xl, norm2_g[l]), sh2_l, sc2_l), w_ff1[l], w_ff2[l])
        if not last:
            yc = _mixer_out(pc, h_c, None, conv_dw_w[l], conv_ln_g[l], conv_ln_b[l], w_conv_out[l],
                            mlstm_norm_g[l], w_mlstm_out[l], w_out[l])
            xc = xc + ga1_c * yc
            xc = xc + ga2_c * _ffn(_modulate(_rmsnorm(xc, norm2_g[l]), sh2_c, sc2_c), w_ff1[l], w_ff2[l])
    return _rmsnorm(xl, final_g)
```

```python
import contextlib
import numpy as np
import concourse.bass as bass
import concourse.mybir as mybir
from concourse.bass_utils import run_bass_kernel_spmd

F32 = mybir.dt.float32
BF = mybir.dt.bfloat16
AF = mybir.ActivationFunctionType
ALU = mybir.AluOpType
F32R = mybir.dt.float32r
DSZ = {F32: 4, BF: 2, F32R: 4}

DEPTH = 2
D = 1024
T = 2304
NT = 18
CTX = 256
SEQ = 2048
HEADS = 4
DH = 512
INW = 12304
EPS = 1e-6
M_INIT = -1e30
BLKS = [(0, 256), (256, 768), (768, 1280), (1280, 1792), (1792, 2304)]
C_GA, C_GB, C_Q, C_K, C_V, C_O, C_G, C_BC, C_BM = 0, 1024, 2048, 4096, 6144, 8192, 10240, 10256, 11280
LC = 504
O_BMOD, O_N1, O_N2, O_LNG, O_LNB, O_DW, O_QK, O_MNG = 0, 48, 56, 64, 72, 80, 328, 488
O_FG = DEPTH * LC
O_C = O_FG + 8
NCOL = O_C + 16

COMPUTE = ("pe", "act", "dve", "pool")


def box_of(ap):
    pat = ap.ap
    off = ap.offset
    name = ap.tensor.name
    sz = DSZ[ap.dtype]
    if "DRAM" in str(ap.space).upper():
        hi = off
        for st, n in pat:
            if st > 0:
                hi += st * (n - 1)
        return (name, 0, 1, off * sz, (hi + 1) * sz)
    if name.startswith("ps"):
        return (name, 0, 128, 0, 2048)
    pstep, pn = pat[0]
    if pstep == 0:
        pstep = 1 << 40
    p0 = off // pstep
    f0 = off % pstep
    f1 = f0
    for st, n in pat[1:]:
        if st > 0:
            f1 += st * (n - 1)
    return (name, p0, p0 + pn, f0 * sz, (f1 + 1) * sz)


def _ovl(a, b):
    return a[1] < b[2] and b[1] < a[2] and a[3] < b[4] and b[3] < a[4]


def _cov(a, b):
    return a[1] <= b[1] and a[2] >= b[2] and a[3] <= b[3] and a[4] >= b[4]


class Op:
    __slots__ = ("eng", "fn", "idx", "waits", "signal", "is_dma", "sem_slot", "target", "vc", "gid")


class Sched:
    def __init__(self, nc, n_dma_sems=24):
        self.nc = nc
        self.ops = {e: [] for e in COMPUTE + ("sp",)}
        self.track = {}
        self.known = {e: {f: -1 for f in COMPUTE} for e in COMPUTE + ("sp",)}
        self.known_dma = {e: set() for e in COMPUTE + ("sp",)}
        self.n_dma_sems = n_dma_sems
        self.dma_slot_last = {}
        self.dma_count = {"sp": 0, "pool": 0}
        self.dma_slot_uses = {}
        self.out_dmas = []
        self.gid = 0

    def _need(self, op, prod):
        e = op.eng
        if prod.is_dma:
            if prod.gid in self.known_dma[e]:
                return
            self.known_dma[e].add(prod.gid)
            op.waits.append(("dma", prod))
            return
        f = prod.eng
        if f == e:
            if e == "pe":
                return
            if self.known[e][e] >= prod.idx:
                return
            self.known[e][e] = prod.idx
            prod.signal = True
            op.waits.append(("eng", prod))
            return
        if self.known[e][f] >= prod.idx:
            return
        prod.signal = True
        op.waits.append(("eng", prod))
        kn = self.known[e]
        kn[f] = prod.idx
        for g, v in prod.vc.items():
            if g != e and kn[g] < v:
                kn[g] = v

    def add(self, eng, fn, reads=(), writes=(), dma=False, out_dma=False):
        op = Op()
        op.eng = eng
        op.fn = fn
        op.idx = len(self.ops[eng])
        op.waits = []
        op.signal = False
        op.is_dma = dma
        op.gid = self.gid
        self.gid += 1
        rb = [box_of(a) for a in reads]
        wb = [box_of(a) for a in writes]
        deps = []
        for b in rb:
            excl = b[0].startswith("ps")
            for (bb, o, k) in self.track.get(b[0], ()):
                if (k == "w" or (excl and o.eng != eng)) and _ovl(b, bb):
                    deps.append(o)
        for b in wb:
            for (bb, o, k) in self.track.get(b[0], ()):
                if _ovl(b, bb):
                    deps.append(o)
        if dma:
            q = eng
            c = self.dma_count[q]
            self.dma_count[q] = c + 1
            slot = c % self.n_dma_sems
            op.sem_slot = (q, slot)
            u = self.dma_slot_uses.get((q, slot), 0) + 1
            self.dma_slot_uses[(q, slot)] = u
            op.target = 16 * u
            prev = self.dma_slot_last.get((q, slot))
            if prev is not None:
                deps.append(prev)
            self.dma_slot_last[(q, slot)] = op
            if out_dma:
                self.out_dmas.append(op)
        best = {}
        for o in deps:
            if o is op:
                continue
            if o.is_dma:
                best[("d", o.gid)] = o
            else:
                cur = best.get(o.eng)
                if cur is None or cur.idx < o.idx:
                    best[o.eng] = o
        for o in best.values():
            self._need(op, o)
        op.vc = dict(self.known[eng])
        if not dma and eng in COMPUTE:
            op.vc[eng] = op.idx - 1
        for b in wb:
            lst = self.track.setdefault(b[0], [])
            lst[:] = [t for t in lst if not _cov(b, t[0])]
            lst.append((b, op, "w"))
        for b in rb:
            lst = self.track.setdefault(b[0], [])
            if not dma:
                lst[:] = [t for t in lst if not (t[2] == "r" and t[1].eng == eng and not t[1].is_dma and _cov(b, t[0]))]
            lst.append((b, op, "r"))
        self.ops[eng].append(op)
        return op

    def emit(self):
        nc = self.nc
        with contextlib.ExitStack() as st:
            esem = {e: st.enter_context(nc.semaphore("s_" + e)) for e in COMPUTE}
            dsem = {}
            for q in ("sp", "pool"):
                for s in range(min(self.n_dma_sems, self.dma_count[q])):
                    dsem[(q, s)] = st.enter_context(nc.semaphore("d_%s_%d" % (q, s)))
            cnt = {}
            for e in COMPUTE:
                c = 0
                for op in self.ops[e]:
                    if op.signal and not op.is_dma:
                        c += 1
                    cnt[op.gid] = c
            block = st.enter_context(nc.Block())

            def run(e, eng):
                for op in self.ops[e]:
                    for kind, prod in op.waits:
                        if kind == "dma":
                            eng.wait_ge(dsem[prod.sem_slot], prod.target)
                        else:
                            eng.wait_ge(esem[prod.eng], cnt[prod.gid])
                    ins = op.fn(eng)
                    if op.is_dma:
                        ins.then_inc(dsem[op.sem_slot], 16)
                    elif op.signal:
                        ins.then_inc(esem[e], 1)
                if e == "sp":
                    for od in self.out_dmas:
                        eng.wait_ge(dsem[od.sem_slot], od.target)

            @block.tensor
            def _(eng):
                run("pe", eng)

            @block.scalar
            def _(eng):
                run("act", eng)

            @block.vector
            def _(eng):
                run("dve", eng)

            @block.gpsimd
            def _(eng):
                run("pool", eng)

            @block.sync
            def _(eng):
                run("sp", eng)


ARENA_BYTES = 212800


class Builder:
    def __init__(self, n_layers=DEPTH, dbg=(), stop=None):
        self.n_layers = n_layers
        self.dbg_req = dict(dbg)
        self.stop = stop
        self.nc = bass.Bass("TRN2", target_bir_lowering=False)
        self.bank_i = 0
        self.ring_i = 0
        self.flip = 0

    def mm(self, out, lhsT, rhs, start=True, stop=True):
        self.S.add("pe", lambda e: e.matmul(out, lhsT=lhsT, rhs=rhs, start=start, stop=stop),
                   reads=[lhsT, rhs], writes=[out])

    def tr(self, out, in_, ident):
        self.S.add("pe", lambda e: e.transpose(out, in_, ident), reads=[in_, ident], writes=[out])

    def act(self, out, in_, func, bias=None, scale=None, accum_out=None):
        kw = {}
        rd = [in_]
        wr = [out]
        if bias is not None:
            kw["bias"] = bias
            if not isinstance(bias, (int, float)):
                rd.append(bias)
        if scale is not None:
            kw["scale"] = scale
            if not isinstance(scale, (int, float)):
                rd.append(scale)
        if accum_out is not None:
            kw["accum_out"] = accum_out
            wr.append(accum_out)
        self.S.add("act", lambda e: e.activation(out=out, in_=in_, func=func, **kw), reads=rd, writes=wr)

    def tt(self, eng, out, in0, in1, op):
        self.S.add(eng, lambda e: e.tensor_tensor(out=out, in0=in0, in1=in1, op=op), reads=[in0, in1], writes=[out])

    def ts(self, eng, out, in0, s1, op0, s2=None, op1=None):
        rd = [in0]
        if not isinstance(s1, (int, float)):
            rd.append(s1)
        if s2 is not None and not isinstance(s2, (int, float)):
            rd.append(s2)
        if op1 is None:
            self.S.add(eng, lambda e: e.tensor_scalar(out=out, in0=in0, scalar1=s1, scalar2=None, op0=op0),
                       reads=rd, writes=[out])
        else:
            self.S.add(eng, lambda e: e.tensor_scalar(out=out, in0=in0, scalar1=s1, scalar2=s2, op0=op0, op1=op1),
                       reads=rd, writes=[out])

    def stt(self, out, in0, scalar, in1, op0, op1):
        rd = [in0, in1]
        if not isinstance(scalar, (int, float)):
            rd.append(scalar)
        self.S.add("dve", lambda e: e.scalar_tensor_tensor(out=out, in0=in0, scalar=scalar, in1=in1, op0=op0, op1=op1),
                   reads=rd, writes=[out])

    def copy(self, eng, out, in_):
        if eng == "act":
            self.S.add("act", lambda e: e.copy(out=out, in_=in_), reads=[in_], writes=[out])
        else:
            self.S.add(eng, lambda e: e.tensor_copy(out=out, in_=in_), reads=[in_], writes=[out])

    def recip(self, out, in_):
        self.S.add("dve", lambda e: e.reciprocal(out=out, in_=in_), reads=[in_], writes=[out])

    def memset(self, eng, out, val):
        self.S.add(eng, lambda e: e.memset(out, val), reads=[], writes=[out])

    def scan(self, out, d0, d1, init, op0, op1):
        self.S.add("dve", lambda e: e.tensor_tensor_scan(out=out, data0=d0, data1=d1, initial=init, op0=op0, op1=op1),
                   reads=[d0, d1], writes=[out])

    def dma(self, q, out, in_, out_dma=False):
        self.S.add(q, lambda e: e.dma_start(out=out, in_=in_), reads=[in_], writes=[out], dma=True, out_dma=out_dma)

    def alt(self):
        self.flip ^= 1
        return "act" if self.flip else "dve"

    def bank(self):
        b = self.psb[self.bank_i % 8]
        self.bank_i += 1
        return b[:]

    def view(self, off, shape, dt, parts=128):
        n = 1
        for s in shape:
            n *= s
        nb = n * DSZ[dt]
        assert off % 4 == 0 and off + nb <= ARENA_BYTES, (off, nb)
        a = self.arena[0:parts, off // 4:(off + nb + 3) // 4]
        if dt == BF:
            a = a.bitcast(BF)
            if a.shape[1] != n:
                a = a[:, 0:n]
        if len(shape) == 2:
            a = a.rearrange("p (a b) -> p a b", a=shape[0])
        elif len(shape) == 3:
            a = a.rearrange("p (a b c) -> p a b c", a=shape[0], b=shape[1])
        return a

    def load_w(self, src, key=None):
        pre = getattr(self, "preloaded", None)
        if key is not None and pre and key in pre:
            return pre.pop(key)
        slot = self.ring[self.ring_i % len(self.ring)]
        self.ring_i += 1
        self.dma("pool", slot, src.rearrange("(kc p) n -> p kc n", p=128))
        return slot

    def dump(self, name, ap, shape):
        if name not in self.dbg_req:
            return
        dt = ap.dtype
        d = self.nc.dram_tensor("dbg_" + name, list(shape), dt, kind="ExternalOutput").ap()
        self.dma("sp", d, ap, out_dma=True)
        self.dbg_out.append("dbg_" + name)

    def build(self):
        nc = self.nc
        L = DEPTH
        self.dbg_out = []
        dr = lambda n, s, dt=F32, k="ExternalInput": nc.dram_tensor(n, list(s), dt, kind=k).ap()
        self.xin = dr("xin", [D, T])
        self.cols_d = dr("cols", [128, NCOL])
        self.consts_d = dr("consts", [128, 5 * 128])
        self.w_mod = dr("w_mod", [L, D, 6 * D])
        self.w_in = dr("w_in", [L, D, INW])
        self.gate_b = dr("gate_b", [L, 16])
        self.mng = dr("mng", [L, 2 * D])
        self.w_conv_out = dr("w_conv_out", [L, D, D])
        self.w_m_out = dr("w_m_out", [L, 2 * D, D])
        self.w_out = dr("w_out", [L, D, D])
        self.w_ff1 = dr("w_ff1", [L, D, 4 * D])
        self.w_ff2 = dr("w_ff2", [L, 4 * D, D])
        self.outT = dr("outT", [D, SEQ], F32, "ExternalOutput")
        self.xscr = dr("xscr", [D, T], F32, "Internal")
        self.hmscr = dr("hmscr", [2 * D, T], BF, "Internal")

        with contextlib.ExitStack() as st:
            self.arena = st.enter_context(nc.sbuf_tensor("arena", [128, ARENA_BYTES // 4], F32))
            self.psb = [st.enter_context(nc.psum_tensor("ps%d" % i, [128, 512], F32)) for i in range(8)]
            self.S = Sched(nc)
            self.layout()
            self.setup()
            for l in range(self.n_layers):
                self.layer(l)
                if self.stop is not None and self.stop[0] == l:
                    break
            self.S.emit()
        return nc

    def layout(self):
        o = 0

        def take(nbytes):
            nonlocal o
            r = o
            o += (nbytes + 3) // 4 * 4
            return r

        V = self.view
        self.hT = V(take(8 * T * 2), (8, T), BF)
        self.ring = [V(take(8192), (8, 512), BF) for _ in range(3)]
        self.cols = V(take(NCOL * 4), (NCOL,), F32)
        cf = take(5 * 512)
        self.ident_f = V(cf, (128,), F32)
        self.J_f = V(cf + 512, (128,), F32)
        self.mask = [V(cf + 1024, (128,), F32), V(cf + 1536, (128,), F32)]
        self.ones_f = V(cf + 2048, (128,), F32)
        self.ident_bf = V(take(256), (128,), BF)
        self.ones_bf = V(take(256), (128,), BF)
        self.negh = V(take(4), (1,), F32)
        self.ones_r = V(take(512), (128,), F32).bitcast(F32R)
        self.msets = [(V(take(96 * 4), (48, 2), F32), V(take(64), (8, 2), F32), V(take(64), (8, 2), F32)) for _ in range(2)]
        self.sT = V(take(32), (8, 2), BF)
        self.stmp = V(take(64), (8, 2), F32)
        self.gb_bc = V(take(64), (16,), F32)
        self.G = V(take(NT * 16 * 4), (NT, 16), F32)
        self.Wg = V(take(8 * 16 * 2), (8, 16), BF)
        self.WS = [V(take(288), (72,), F32) for _ in range(2)]
        self.EBA = [V(take(288), (72,), F32) for _ in range(2)]
        self.WE = [V(take(288), (72,), F32) for _ in range(2)]
        self.base = o
        o = self.base
        self.qT = V(take(4 * T * 2), (4, T), BF)
        self.kT = V(take(4 * T * 2), (4, T), BF)
        self.v = V(take(NT * 512 * 2), (NT, 512), BF)
        hp = take(NT * 512 * 2)
        self.hpart = V(hp, (NT, 512), BF)
        self.raw = V(hp, (4, T), BF)
        self.C = [[V(take(4 * 512 * 4), (4, 512), F32) for _ in range(2)] for _ in range(2)]
        csb = take(4 * 4096)
        self.Csb = [[V(csb + 4096 * (2 * i + j), (4, 512), BF) for j in range(2)] for i in range(2)]
        self.dg5 = V(csb, (20, 128), BF)
        self.nst = [[V(take(16), (4,), F32) for _ in range(2)] for _ in range(2)]
        self.nsb = [[V(take(8), (4,), BF) for _ in range(2)] for _ in range(2)]
        self.kw = [[V(take(1024), (512,), BF) for _ in range(2)] for _ in range(2)]
        self.PT = [[V(take(256), (128,), BF) for _ in range(3)] for _ in range(2)]
        self.dd = [V(take(4), (1,), F32) for _ in range(2)]
        self.rr = [V(take(4), (1,), F32) for _ in range(2)]
        self.hsum = V(take(2048), (512,), F32)
        self.ssq = V(take(4), (1,), F32)
        self.rs1 = V(take(4), (1,), F32)
        self.rs2 = V(take(4), (1,), F32)
        sg_o = take(2048)
        self.sg = V(sg_o, (512,), F32)
        self.hmgs = [V(take(1024), (512,), BF) for _ in range(2)]
        self.junk = V(take(1024), (512,), BF)
        self.hst = [V(take(1024), (4, 128), BF) for _ in range(2)]
        self.sgt = V(sg_o, (512,), F32)
        self.gset = []
        o_save = o
        o = csb
        for _d in range(2):
            g = {}
            g["ct"] = [V(take(512), (128,), F32, parts=72) for _ in range(8)]
            g["icont"] = V(take(288), (72,), F32)
            g["fcont"] = V(take(288), (72,), F32)
            g["rows"] = V(take(3 * 288), (3, 72), F32, parts=1)
            g["mtmp"] = V(take(16), (4,), F32, parts=1)
            g["c72"] = [V(take(4), (1,), F32, parts=72) for _ in range(5)]
            g["dgw"] = V(take(288), (72,), F32, parts=72)
            g["t128"] = V(take(288), (72,), F32)
            self.gset.append(g)
            self.gset.append(g)
            break
        assert o <= csb + 8192
        o = o_save
        self.scan_end = o
        assert o <= ARENA_BYTES, o
        o = self.base
        self.xb = V(take(8 * 512 * 4), (8, 512), F32)
        sq_o = take(8 * 512 * 4)
        self.sq = V(sq_o, (8, 512), BF)
        self.Vb = V(sq_o + 8192, (8, 512), BF)
        self.Vf = V(take(8 * 512 * 4), (8, 512), F32)
        self.rstd = V(take(2048), (512,), F32)
        self.t1 = V(take(2048), (512,), F32)
        self.t2 = V(take(2048), (512,), F32)
        self.t3 = V(take(2048), (512,), F32)
        ub = take(8 * 656 * 2)
        self.U = V(ub, (8, 656), BF)
        zb = take(8 * 512 * 2)
        self.Z = V(zb, (8, 512), BF)
        self.Ybf = V(ub, (8, 512), BF)
        self.Y = V(take(8 * 512 * 4), (8, 512), F32)
        self.Hd = V(ub, (32, 512), BF)
        hb = take(16 * 512 * 2)
        self.dg31 = [V(hb + 7936 * i, (31, 128), BF) for i in range(2)]
        self.S8 = V(hb, (8, 512), BF)
        self.S7 = V(hb + 8192, (8, 512), BF)
        self.HM = V(self.base + 16384, (16, 512), BF)
        self.h2T = V(take(8 * 512 * 2), (8, 512), BF)
        self.cacc = V(take(656 * 4), (656,), F32)
        self.ring_extra = [V(take(8192), (8, 512), BF) for _ in range(2)]
        assert o <= ARENA_BYTES, o
        self.final_end = o

    def setup(self):
        self.dma("sp", self.cols, self.cols_d)
        cf = self.consts_d
        self.dma("sp", self.ident_f, cf[:, 0:128])
        self.dma("sp", self.J_f, cf[:, 128:256])
        self.dma("sp", self.mask[0], cf[:, 256:384])
        self.dma("sp", self.mask[1], cf[:, 384:512])
        self.dma("sp", self.ones_f, cf[:, 512:640])
        self.copy("dve", self.ident_bf, self.ident_f)
        self.copy("dve", self.ones_bf, self.ones_f)
        self.memset("dve", self.negh, -0.5)

    def rmsnorm_block(self, xb, nt, gs, sh, which, out):
        for kc in range(8):
            self.act(self.sq[:, kc, 0:nt], xb[:, kc, 0:nt], AF.Square)
        ps = self.bank()
        for kc in range(8):
            self.mm(ps[:, 0:nt], self.ones_bf, self.sq[:, kc, 0:nt], start=(kc == 0), stop=(kc == 7))
        self.ts("dve", self.t1[:, 0:nt], ps[:, 0:nt], 1.0 / D, ALU.mult, EPS, ALU.add)
        self.act(self.t2[:, 0:nt], self.t1[:, 0:nt], AF.Sqrt)
        self.recip(self.rstd[:, 0:nt], self.t2[:, 0:nt])
        for kc in range(8):
            tmp = (self.t3, self.t1, self.t2)[kc % 3]
            self.stt(tmp[:, 0:nt], xb[:, kc, 0:nt], gs[:, kc, which:which + 1], self.rstd[:, 0:nt], ALU.mult, ALU.mult)
            if sh is None:
                self.copy("act", out[:, kc, 0:nt], tmp[:, 0:nt])
            else:
                self.act(out[:, kc, 0:nt], tmp[:, 0:nt], AF.Identity, bias=sh[:, kc, which:which + 1], scale=1.0)

    def mod_piece(self, l, u):
        cols = self.cols
        cb = l * LC
        mod, g1s, g2s = self.msets[l % 2]
        if u == 0:
            cc = cols[:, O_C:O_C + 16].rearrange("p (a b) -> p a b", a=8)
            self.act(self.stmp, cc, AF.Sigmoid)
            self.tt("dve", self.sT, self.stmp, cc, ALU.mult)
        psm = self.bank()
        W = self.load_w(self.w_mod[l][:, u * 512:(u + 1) * 512])
        for j in range(4):
            for kc in range(8):
                self.mm(psm[:, j * 2:j * 2 + 2], W[:, kc, j * 128:(j + 1) * 128], self.sT[:, kc, :],
                        start=(kc == 0), stop=(kc == 7))
        bm = cols[:, cb + O_BMOD + u * 4:cb + O_BMOD + u * 4 + 4]
        self.tt("dve", mod[:, u * 4:(u + 1) * 4, :], psm[:, 0:8].rearrange("p (a b) -> p a b", b=2),
                bm.unsqueeze(2).to_broadcast([128, 4, 2]), ALU.add)
        if u == 3:
            sc1 = mod[:, 8:16, :]
            n1 = cols[:, cb + O_N1:cb + O_N1 + 8].unsqueeze(2).to_broadcast([128, 8, 2])
            self.ts("dve", g1s, sc1, 1.0, ALU.add)
            self.tt("dve", g1s, g1s, n1, ALU.mult)
        if u == 9:
            sc2 = mod[:, 32:40, :]
            n2 = cols[:, cb + O_N2:cb + O_N2 + 8].unsqueeze(2).to_broadcast([128, 8, 2])
            self.ts("dve", g2s, sc2, 1.0, ALU.add)
            self.tt("dve", g2s, g2s, n2, ALU.mult)
        if u == 11:
            self.dump("mod%d" % l, mod, [128, 48, 2])

    def modulation(self, l):
        for u in range(12):
            self.mod_piece(l, u)

    def layer(self, l):
        last = (l == DEPTH - 1)
        cb = l * LC
        cols = self.cols
        xsrc = self.xin if l == 0 else self.xscr
        fused = self.stop is None
        if l == 0:
            for u in range(4):
                self.mod_piece(0, u)
            self.mod_heads = [(0, u) for u in range(4, 12)]
        else:
            self.mod_heads = [] if fused else []
        if l + 1 < self.n_layers and fused:
            self.mod_next_early = [(l + 1, u) for u in range(4)]
            self.mod_todo = [(l + 1, u) for u in range(4, 12)]
        else:
            self.mod_next_early = []
            self.mod_todo = []
        if l > 0 and not fused:
            self.modulation(l)
        mod, g1s, g2s = self.msets[l % 2]
        self.g1s, self.g2s = g1s, g2s
        sh1, sc1, ga1, sh2, sc2, ga2 = [mod[:, m * 8:(m + 1) * 8, :] for m in range(6)]

        if l == 0 or not fused:
            for bi, (t0, t1) in enumerate(BLKS):
                nt = t1 - t0
                which = 1 if bi == 0 else 0
                xb = (self.xb, self.Vf)[bi % 2]
                self.dma("sp", xb[:, :, 0:nt], xsrc[:, t0:t1].rearrange("(kc p) t -> p kc t", p=128))
                self.rmsnorm_block(xb, nt, self.g1s, sh1, which, self.hT[:, :, t0:t1])
        self.dump("hT%d" % l, self.hT, [128, 8, T])
        if self.stop == (l, "B"):
            return

        self.gates(l)
        if self.stop == (l, "C"):
            self.gates2(l)
            return

        for h in range(HEADS):
            self.head(l, h, last)
            if self.stop == (l, "D%d" % h):
                return

        ring3 = self.ring
        self.ring = ring3 + self.ring_extra
        for bi, (t0, t1) in enumerate(BLKS):
            if last and bi == 0:
                continue
            self.final_block(l, bi, t0, t1, last, sh1, sc1, ga1, sh2, sc2, ga2)
            if self.stop == (l, "E%d" % bi):
                return
        self.ring = ring3

    def gates(self, l):
        self.dma("pool", self.Wg, self.w_in[l][:, C_G:C_G + 16].rearrange("(kc p) n -> p kc n", p=128))
        self.dma("sp", self.gb_bc, self.gate_b[l].partition_broadcast(128))
        psg = self.bank()
        for tl in range(NT):
            for kc in range(8):
                self.mm(psg[:, tl * 16:(tl + 1) * 16], self.hT[:, kc, tl * 128:(tl + 1) * 128], self.Wg[:, kc, :],
                        start=(kc == 0), stop=(kc == 7))
        self.tt("dve", self.G, psg[:, 0:NT * 16].rearrange("p (a b) -> p a b", b=16),
                self.gb_bc.unsqueeze(1).to_broadcast([128, NT, 16]), ALU.add)
        self.dump("G%d" % l, self.G, [128, NT, 16])
        self.pending_gates = l

    def gates2(self, l):
        import os
        gs0 = gs = float(os.environ.get("GSTOP", "99"))
        if gs <= 1:
            return
        id72 = self.ident_f[0:72, 0:72]
        gdirs = [int(x) for x in os.environ.get("GDIRS", "0,1").split(",")]
        for gi, d in enumerate(gdirs):
            R = self.ident_f if d == 0 else self.J_f
            gs = gs0 if gi == len(gdirs) - 1 else 99
            self.bank_i += int(os.environ.get("GBUMP", "0")) if gi > 0 else 0
            gq = self.gset[gi % 2]
            self.ct, self.icont, self.fcont, self.rows = gq["ct"], gq["icont"], gq["fcont"], gq["rows"]
            self.mtmp, self.c72, self.dgw, self.t128 = gq["mtmp"], gq["c72"], gq["dgw"], gq["t128"]
            cti, ctf, sp, cs, a, cm, wsc, ebc = self.ct
            ic3 = self.icont.rearrange("p (a b) -> p a b", b=4)
            fc3 = self.fcont.rearrange("p (a b) -> p a b", b=4)
            self.copy("dve", ic3, self.G[:, :, d * 8:d * 8 + 4])
            self.copy("dve", fc3, self.G[:, :, d * 8 + 4:d * 8 + 8])
            if gs <= 1.2:
                return
            ps = self.bank()
            self.mm(ps[0:72, 0:128], self.icont, R)
            self.mm(ps[0:72, 128:256], self.fcont, R)
            if gs <= 1.5:
                return
            self.copy("dve", cti, ps[0:72, 0:128])
            if gs <= 1.7:
                return
            zc = self.c72[4]
            self.memset("dve", zc, 0.0)
            self.copy("dve", ctf, ps[0:72, 128:256])
            self.act(sp, ctf, AF.Exp, bias=zc, scale=-1.0)
            if gs <= 1.8:
                return
            self.act(sp, sp, AF.Ln, bias=1.0)
            if gs <= 2:
                return
            self.scan(cs, self.ones_f[0:72, :], sp, 0.0, ALU.mult, ALU.add)
            self.tt("dve", a, cti, cs, ALU.add)
            self.scan(cm, a, a, M_INIT, ALU.max, ALU.max)
            if gs <= 3:
                return
            ps2 = self.bank()
            self.mm(ps2[0:1, 0:72], cm[:, 127:128], id72)
            self.mm(ps2[0:1, 72:144], cs[:, 127:128], id72)
            rows = self.rows
            self.copy("dve", rows[:, 0, :], ps2[0:1, 0:72])
            self.copy("dve", rows[:, 1, :], ps2[0:1, 72:144])
            order = list(range(NT)) if d == 0 else [1, 0] + list(range(NT - 1, 1, -1))
            mrow = rows[:, 2, :]
            self.memset("dve", mrow[:, order[0] * 4:order[0] * 4 + 4], M_INIT)
            for i in range(NT - 1):
                c, n = order[i], order[i + 1]
                self.tt("dve", self.mtmp, mrow[:, c * 4:c * 4 + 4], rows[:, 0, c * 4:c * 4 + 4], ALU.max)
                self.tt("dve", mrow[:, n * 4:n * 4 + 4], self.mtmp, rows[:, 1, c * 4:c * 4 + 4], ALU.subtract)
            if gs <= 4:
                return
            ps3 = self.bank()
            self.mm(ps3[0:72, 0:1], mrow, self.ones_f[0:1, 0:1])
            mst, A, negA, wec, _ = self.c72
            self.copy("dve", mst, ps3[0:72, 0:1])
            self.tt("dve", A, mst, cm[:, 127:128], ALU.max)
            self.ts("dve", negA, A, -1.0, ALU.mult)
            self.act(wsc, a, AF.Exp, bias=negA, scale=1.0)
            self.act(ebc, cs, AF.Exp, bias=negA, scale=1.0)
            self.act(wec, mst, AF.Exp, bias=negA, scale=1.0)
            if gs <= 5:
                return
            for src, dst in ((wsc, self.WS[d]), (ebc, self.EBA[d])):
                ps4 = self.bank()
                self.mm(ps4[:, 0:72], src, id72)
                if d == 0:
                    self.copy("dve", dst, ps4[:, 0:72])
                else:
                    self.copy("dve", self.t128, ps4[:, 0:72])
                    ps5 = self.bank()
                    self.mm(ps5[:, 0:72], self.J_f, self.t128)
                    self.copy("dve", dst, ps5[:, 0:72])
            if gs <= 6:
                return
            self.ts("dve", self.dgw, id72, wec, ALU.mult)
            ps6 = self.bank()
            self.mm(ps6[:, 0:72], self.ones_f[0:72, :], self.dgw)
            self.copy("dve", self.WE[d], ps6[:, 0:72])
            if gs <= 7:
                return
            self.dump("WS%d_%d" % (l, d), self.WS[d], [128, 72])
            self.dump("EBA%d_%d" % (l, d), self.EBA[d], [128, 72])
            self.dump("WE%d_%d" % (l, d), self.WE[d], [128, 72])

    def head(self, l, h, last):
        cols = self.cols
        cb = l * LC
        for wi, (c0, dst, scl) in enumerate(((C_Q, self.qT, 1.0), (C_K, self.kT, DH ** -0.5))):
            W = self.load_w(self.w_in[l][:, c0 + h * 512:c0 + (h + 1) * 512], key=(l, h, wi))
            for dc in range(4):
                ch = (0 if wi == 0 else 16) + h * 4 + dc
                for o in range(5):
                    wc = cols[:, cb + O_QK + ch * 5 + o:cb + O_QK + ch * 5 + o + 1]
                    self.ts("pool", self.dg5[:, dc * 5 + o, :], self.ident_bf, wc, ALU.mult, 0.0, ALU.add)
            for (t0, t1) in BLKS:
                nt = t1 - t0
                for dc in range(4):
                    ps = self.bank()
                    for kc in range(8):
                        self.mm(ps[:, 0:nt], W[:, kc, dc * 128:(dc + 1) * 128], self.hT[:, kc, t0:t1],
                                start=(kc == 0), stop=(kc == 7))
                    self.copy(self.alt(), self.raw[:, dc, t0:t1], ps[:, 0:nt])
            for bi, (t0, t1) in enumerate(BLKS):
                nt = t1 - t0
                s0, s1 = (0, CTX) if bi == 0 else (CTX, T)
                for dc in range(4):
                    ps = self.bank()
                    taps = [0, -2, -1, 1, 2]
                    for ti, o in enumerate(taps):
                        jl = max(t0, s0 - o)
                        jh = min(t1, s1 - o)
                        self.mm(ps[:, jl - t0:jh - t0], self.dg5[:, dc * 5 + (o + 2), :], self.raw[:, dc, jl + o:jh + o],
                                start=(ti == 0), stop=(ti == 4))
                    self.act(self.sgt[:, 0:nt], ps[:, 0:nt], AF.Sigmoid)
                    self.stt(dst[:, dc, t0:t1], ps[:, 0:nt], scl, self.sgt[:, 0:nt], ALU.mult, ALU.mult)
        if h == 0:
            self.dump("qT%d" % l, self.qT, [128, 4, T])
            self.dump("kT%d" % l, self.kT, [128, 4, T])
        W = self.load_w(self.w_in[l][:, C_V + h * 512:C_V + (h + 1) * 512])
        for tl in range(NT):
            ps = self.bank()
            for kc in range(8):
                self.mm(ps, self.hT[:, kc, tl * 128:(tl + 1) * 128], W[:, kc, :], start=(kc == 0), stop=(kc == 7))
            self.copy(self.alt(), self.v[:, tl, :], ps)
        if h == 0:
            self.dump("v%d" % l, self.v, [128, NT, 512])
        for _ in range(2):
            if self.mod_heads:
                self.mod_piece(*self.mod_heads.pop(0))
        if self.mod_next_early:
            self.mod_piece(*self.mod_next_early.pop(0))
        if getattr(self, "pending_gates", None) is not None:
            self.gates2(self.pending_gates)
            self.pending_gates = None
        Wo = self.load_w(self.w_in[l][:, C_O + h * 512:C_O + (h + 1) * 512])
        if h + 1 < HEADS:
            self.preloaded = {}
            for wi, c0 in enumerate((C_Q, C_K)):
                self.preloaded[(l, h + 1, wi)] = self.load_w(self.w_in[l][:, c0 + (h + 1) * 512:c0 + (h + 2) * 512])
        orders = [list(range(NT)), [1, 0] + list(range(NT - 1, 1, -1))]
        visited = set()
        self.hst_i = 0
        self.pending = []
        qT, kT, v = self.qT, self.kT, self.v
        need = lambda tl: not (last and tl < 2)
        psK = {}
        psn = {}

        def cols_of(d, i):
            tl = orders[d][i]
            col = tl * 4 + h
            return (tl, slice(tl * 128, (tl + 1) * 128), self.WS[d][:, col:col + 1], self.EBA[d][:, col:col + 1],
                    self.WE[d][:, col:col + 1])

        def stA(i):
            if i >= NT - 1:
                return
            for d in range(2):
                tl, tok, ws, eba, we = cols_of(d, i)
                psT = self.bank()
                psTb = psT.bitcast(BF)
                for dc in range(4):
                    self.tr(psTb[:, dc * 128:(dc + 1) * 128], kT[:, dc, tok], self.ident_bf)
                self.act(self.kw[d][i % 2], psTb[:, 0:512], AF.Copy, scale=ws)

        def stB(i):
            for d in range(2):
                tl, tok, ws, eba, we = cols_of(d, i)
                if not need(tl):
                    continue
                psS = self.bank()
                for dc in range(4):
                    self.mm(psS[:, 0:128], kT[:, dc, tok], qT[:, dc, tok], start=(dc == 0), stop=(dc == 3))
                self.stt(self.PT[d][i % 3], psS[:, 0:128], ws, self.mask[d], ALU.mult, ALU.mult)

        def stC(i, d):
            if True:
                tl, tok, ws, eba, we = cols_of(d, i)
                r = i % 2
                if i < NT - 1:
                    kw = self.kw[d][i % 2]
                    for dc in range(4):
                        psK[(d, dc)] = self.bank()
                        self.mm(psK[(d, dc)], kw[:, dc * 128:(dc + 1) * 128], v[:, tl, :])
                    psn[d] = self.bank()
                    for dc in range(4):
                        self.mm(psn[d][:, dc:dc + 1], kw[:, dc * 128:(dc + 1) * 128], self.ones_bf[:, 0:1])
                if i > 0 and need(tl):
                    C = self.C[d][r]
                    self.act(self.Csb[d][r][:, 0:2, :], C[:, 0:2, :], AF.Copy, scale=we)
                    self.ts("pool", self.Csb[d][r][:, 2:4, :], C[:, 2:4, :], we, ALU.mult, 0.0, ALU.add)
                    self.ts("pool", self.nsb[d][r], self.nst[d][r], we, ALU.mult, 0.0, ALU.add)

        def stD(i, d):
            if i >= NT - 1:
                return
            if True:
                tl, tok, ws, eba, we = cols_of(d, i)
                r, w = i % 2, 1 - i % 2
                for dc in range(4):
                    if i == 0:
                        self.copy("dve", self.C[d][w][:, dc, :], psK[(d, dc)])
                    else:
                        self.stt(self.C[d][w][:, dc, :], self.C[d][r][:, dc, :], we, psK[(d, dc)], ALU.mult, ALU.add)
                if i == 0:
                    self.copy("dve", self.nst[d][w], psn[d][:, 0:4])
                else:
                    self.stt(self.nst[d][w], self.nst[d][r], we, psn[d][:, 0:4], ALU.mult, ALU.add)

        def stE(i):
            self.out_stage2()
            for d in range(2):
                tl, tok, ws, eba, we = cols_of(d, i)
                if need(tl):
                    PT = self.PT[d][i % 3]
                    Csb, nsb = self.Csb[d][i % 2], self.nsb[d][i % 2]
                    psN = self.bank()
                    psD = self.bank()
                    if i > 0:
                        for dc in range(4):
                            self.mm(psN, qT[:, dc, tok], Csb[:, dc, :], start=(dc == 0), stop=False)
                    self.mm(psN, PT, v[:, tl, :], start=(i == 0), stop=True)
                    if i > 0:
                        for dc in range(4):
                            self.mm(psD[:, 0:1], qT[:, dc, tok], nsb[:, dc:dc + 1], start=(dc == 0), stop=False)
                    self.mm(psD[:, 0:1], PT, self.ones_bf[:, 0:1], start=(i == 0), stop=True)
                    self.ts("dve", self.dd[d], psD[:, 0:1], -1.0, ALU.mult)
                    self.tt("dve", self.dd[d], psD[:, 0:1], self.dd[d], ALU.max)
                    self.tt("dve", self.dd[d], self.dd[d], eba, ALU.max)
                    self.recip(self.rr[d], self.dd[d])
                    if tl not in visited:
                        self.act(self.hpart[:, tl, :], psN, AF.Copy, scale=self.rr[d])
                        if h == 0 and i == 0 and d == 0:
                            self.dump("he%d" % l, self.hpart[:, tl, :], [128, 512])
                            self.dump("rr%d" % l, self.rr[d], [128, 1])
                    else:
                        self.stt(self.hsum, psN, self.rr[d], self.hpart[:, tl, :], ALU.mult, ALU.add)
                        self.out_stage(l, h, tl, Wo)
                visited.add(tl)

        stA(0)
        stB(0)
        if h == 0:
            self.dump("pt%d" % l, self.PT[0][0], [128, 128])
        for i in range(NT):
            for d in range(2):
                stC(i, d)
                stD(i, d)
            if i + 1 < NT:
                stA(i + 1)
                stB(i + 1)
            if i >= 1:
                stE(i - 1)
        stE(NT - 1)
        self.out_stage2()
        if h == 0:
            self.dump("hp%d" % l, self.hpart, [128, NT, 512])

    def out_stage(self, l, h, tl, Wo):
        tok = slice(tl * 128, (tl + 1) * 128)
        self.act(self.junk, self.hsum, AF.Square, accum_out=self.ssq)
        self.ts("dve", self.rs1, self.ssq, 1.0 / DH, ALU.mult, EPS, ALU.add)
        self.tt("pool", self.rs2, self.rs1, self.negh, ALU.pow)
        psO = self.bank()
        for kc in range(8):
            self.mm(psO, self.hT[:, kc, tok], Wo[:, kc, :], start=(kc == 0), stop=(kc == 7))
        self.act(self.sg, psO, AF.Sigmoid)
        hmg = self.hmgs[self.hst_i % 2]
        self.stt(hmg, self.hsum, self.rs2, self.sg, ALU.mult, ALU.mult)
        if h == 0 and tl in (0, 5):
            self.dump("hsum%d_%d" % (l, tl), self.hsum, [128, 512])
        self.pending.append((l, h, tl, hmg, self.hst[self.hst_i % 2]))
        self.hst_i += 1

    def out_stage2(self):
        while self.pending:
            l, h, tl, hmg, hst = self.pending.pop(0)
            tok = slice(tl * 128, (tl + 1) * 128)
            psH = self.bank()
            psHb = psH.bitcast(BF)
            for ec in range(4):
                self.tr(psHb[:, ec * 128:(ec + 1) * 128], hmg[:, ec * 128:(ec + 1) * 128], self.ident_bf)
            cb = l * LC
            for ec in range(4):
                g = self.cols[:, cb + O_MNG + h * 4 + ec:cb + O_MNG + h * 4 + ec + 1]
                self.act(hst[:, ec, :], psHb[:, ec * 128:(ec + 1) * 128], AF.Copy, scale=g)
            if h == 0 and tl in (0, 5):
                self.dump("hmT%d_%d" % (l, tl), hst, [128, 4, 128])
            dst = self.hmscr[h * 512:(h + 1) * 512, tok].rearrange("(ec p) t -> p ec t", p=128)
            self.dma("sp", dst, hst)

    def final_block(self, l, bi, t0, t1, last, sh1, sc1, ga1, sh2, sc2, ga2):
        nt = t1 - t0
        which = 1 if bi == 0 else 0
        cols = self.cols
        cb = l * LC
        xsrc = self.xin if l == 0 else self.xscr
        win = self.w_in[l]
        hTb = self.hT[:, :, t0:t1]
        xb = self.xb
        self.dma("sp", xb[:, :, 0:nt], xsrc[:, t0:t1].rearrange("(kc p) t -> p kc t", p=128))
        rl = 256 if bi == 0 else 64
        nr = nt // rl
        stride = rl + 16
        rpg = 1 if bi == 0 else 4
        gw = rpg * stride
        self.memset("dve", self.U, 0.0)
        for half in range(2):
            Wa = self.load_w(win[:, C_GA + half * 512:C_GA + (half + 1) * 512])
            Wb = self.load_w(win[:, C_GB + half * 512:C_GB + (half + 1) * 512])
            for j in range(4):
                c8 = half * 4 + j
                psA = self.bank()
                psB = self.bank()
                for kc in range(8):
                    self.mm(psA[:, 0:nt], Wa[:, kc, j * 128:(j + 1) * 128], hTb[:, kc, :], start=(kc == 0), stop=(kc == 7))
                for kc in range(8):
                    self.mm(psB[:, 0:nt], Wb[:, kc, j * 128:(j + 1) * 128], hTb[:, kc, :], start=(kc == 0), stop=(kc == 7))
                self.act(self.t1[:, 0:nt], psB[:, 0:nt], AF.Sigmoid)
                self.tt("dve", self.U[:, c8, 0:nr * stride].rearrange("p (r c) -> p r c", c=stride)[:, :, 15:15 + rl],
                        psA[:, 0:nt].rearrange("p (r c) -> p r c", c=rl),
                        self.t1[:, 0:nt].rearrange("p (r c) -> p r c", c=rl), ALU.mult)
        def conv_pe(c8):
            dg = self.dg31[c8 % 2]
            for o in range(31):
                wc = cols[:, cb + O_DW + c8 * 31 + o:cb + O_DW + c8 * 31 + o + 1]
                if o % 2 == 0:
                    self.ts("dve", dg[:, o, :], self.ident_bf, wc, ALU.mult)
                else:
                    self.act(dg[:, o, :], self.ident_bf, AF.Copy, scale=wc)
            self.pump(4)
            for g in range(nr // rpg):
                ps = self.bank()
                b0 = g * gw
                for ti in range(31):
                    o = ti - 15
                    self.mm(ps[:, 15:gw - 1], dg[:, ti, :], self.U[:, c8, b0 + 15 + o:b0 + gw - 1 + o],
                            start=(ti == 0), stop=(ti == 30))
                self.pump(3)
                src = ps[:, 0:gw].rearrange("p (r c) -> p r c", c=stride)[:, :, 15:15 + rl]
                n0, n1 = g * rpg * rl, (g + 1) * rpg * rl
                self.copy("dve", self.Vf[:, c8, n0:n1].rearrange("p (r c) -> p r c", c=rl), src)
                self.act(self.sq[:, c8, n0:n1].rearrange("p (r c) -> p r c", c=rl), src, AF.Square)
                self.act(self.Vb[:, c8, n0:n1].rearrange("p (r c) -> p r c", c=rl), src, AF.Copy)

        def conv_dve(c8):
            acc = self.cacc
            tot = nr * stride
            L = tot - 16
            order = list(range(1, 31)) + [0]
            for n, ti in enumerate(order):
                yield
                o = ti - 15
                wc = cols[:, cb + O_DW + c8 * 31 + ti:cb + O_DW + c8 * 31 + ti + 1]
                if n == 0:
                    self.ts("dve", acc[:, 0:L], self.U[:, c8, 15 + o:15 + o + L], wc, ALU.mult)
                elif n < 30:
                    self.stt(acc[:, 0:L], self.U[:, c8, 15 + o:15 + o + L], wc, acc[:, 0:L], ALU.mult, ALU.add)
                else:
                    self.stt(self.Vf[:, c8, 0:nt].rearrange("p (r c) -> p r c", c=rl),
                             self.U[:, c8, 0:tot].rearrange("p (r c) -> p r c", c=stride)[:, :, 0:rl], wc,
                             acc[:, 0:tot].rearrange("p (r c) -> p r c", c=stride)[:, :, 0:rl], ALU.mult, ALU.add)
            self.act(self.sq[:, c8, 0:nt], self.Vf[:, c8, 0:nt], AF.Square)
            self.act(self.Vb[:, c8, 0:nt], self.Vf[:, c8, 0:nt], AF.Copy)

        pe_ch, dve_ch = [0, 1, 2, 3, 4], [5, 6, 7]
        jobs = [conv_dve(c) for c in dve_ch]

        def pump(n):
            while n > 0 and jobs:
                try:
                    next(jobs[0])
                    n -= 1
                except StopIteration:
                    jobs.pop(0)

        self.pump = pump
        for i, c8 in enumerate(pe_ch):
            conv_pe(c8)
        for half in range(2):
            W8 = self.load_w(win[:, C_BM + half * 512:C_BM + (half + 1) * 512])
            W7 = self.load_w(win[:, C_BC + half * 512:C_BC + (half + 1) * 512])
            for j in range(4):
                c8 = half * 4 + j
                ps8 = self.bank()
                ps7 = self.bank()
                for kc in range(8):
                    self.mm(ps8[:, 0:nt], W8[:, kc, j * 128:(j + 1) * 128], hTb[:, kc, :], start=(kc == 0), stop=(kc == 7))
                for kc in range(8):
                    self.mm(ps7[:, 0:nt], W7[:, kc, j * 128:(j + 1) * 128], hTb[:, kc, :], start=(kc == 0), stop=(kc == 7))
                self.act(self.S8[:, c8, 0:nt], ps8[:, 0:nt], AF.Sigmoid)
                self.act(self.S7[:, c8, 0:nt], ps7[:, 0:nt], AF.Sigmoid)
                self.pump(6)
        self.pump(10 ** 6)
        psM = self.bank()
        psQ = self.bank()
        for c8 in range(8):
            self.mm(psM[:, 0:nt], self.ones_bf, self.Vb[:, c8, 0:nt], start=(c8 == 0), stop=(c8 == 7))
        for c8 in range(8):
            self.mm(psQ[:, 0:nt], self.ones_bf, self.sq[:, c8, 0:nt], start=(c8 == 0), stop=(c8 == 7))
        mean, msq, var = self.t1, self.t2, self.t3
        self.act(mean[:, 0:nt], psM[:, 0:nt], AF.Copy, scale=1.0 / D)
        self.act(var[:, 0:nt], psQ[:, 0:nt], AF.Copy, scale=1.0 / D)
        self.dma("sp", self.HM[:, :, 0:nt], self.hmscr[:, t0:t1].rearrange("(ec p) t -> p ec t", p=128))
        self.tt("dve", msq[:, 0:nt], mean[:, 0:nt], mean[:, 0:nt], ALU.mult)
        self.tt("dve", var[:, 0:nt], var[:, 0:nt], msq[:, 0:nt], ALU.subtract)
        self.ts("dve", var[:, 0:nt], var[:, 0:nt], EPS, ALU.add)
        self.act(msq[:, 0:nt], var[:, 0:nt], AF.Sqrt)
        self.recip(self.rstd[:, 0:nt], msq[:, 0:nt])
        for c8 in range(8):
            self.tt("dve", self.Vf[:, c8, 0:nt], self.Vf[:, c8, 0:nt], mean[:, 0:nt], ALU.subtract)
            self.tt("dve", self.Vf[:, c8, 0:nt], self.Vf[:, c8, 0:nt], self.rstd[:, 0:nt], ALU.mult)
        for c8 in range(8):
            g = cols[:, cb + O_LNG + c8:cb + O_LNG + c8 + 1]
            b = cols[:, cb + O_LNB + c8:cb + O_LNB + c8 + 1]
            self.act(self.Z[:, c8, 0:nt], self.Vf[:, c8, 0:nt], AF.Sigmoid, bias=b, scale=g)
            self.act(self.Vf[:, c8, 0:nt], self.Vf[:, c8, 0:nt], AF.Identity, bias=b, scale=g)
        for c8 in range(8):
            self.tt("dve", self.Z[:, c8, 0:nt], self.Z[:, c8, 0:nt], self.Vf[:, c8, 0:nt], ALU.mult)
        if bi in (0, 1):
            self.dump("Z%d_%d" % (l, bi), self.Z, [128, 8, 512])
        for half in range(2):
            Wm = [self.load_w(self.w_m_out[l][kh * 1024:(kh + 1) * 1024, half * 512:(half + 1) * 512]) for kh in range(2)]
            for j in range(4):
                c8 = half * 4 + j
                psY = self.bank()
                for ec in range(16):
                    self.mm(psY[:, 0:nt], Wm[ec // 8][:, ec % 8, j * 128:(j + 1) * 128], self.HM[:, ec, 0:nt],
                            start=(ec == 0), stop=(ec == 15))
                self.tt("dve", self.Y[:, c8, 0:nt], psY[:, 0:nt], self.S8[:, c8, 0:nt], ALU.mult)
        for half in range(2):
            Wc = self.load_w(self.w_conv_out[l][:, half * 512:(half + 1) * 512])
            for j in range(4):
                c8 = half * 4 + j
                psY = self.bank()
                for kc in range(8):
                    self.mm(psY[:, 0:nt], Wc[:, kc, j * 128:(j + 1) * 128], self.Z[:, kc, 0:nt], start=(kc == 0), stop=(kc == 7))
                tmp = (self.t2, self.t3)[c8 % 2]
                self.tt("dve", tmp[:, 0:nt], psY[:, 0:nt], self.S7[:, c8, 0:nt], ALU.mult)
                self.tt("dve", self.Ybf[:, c8, 0:nt], tmp[:, 0:nt], self.Y[:, c8, 0:nt], ALU.add)
        if bi in (0, 1):
            self.dump("Y%d_%d" % (l, bi), self.Ybf, [128, 8, 512])
        for half in range(2):
            Wo_ = self.load_w(self.w_out[l][:, half * 512:(half + 1) * 512])
            for j in range(4):
                c8 = half * 4 + j
                psZ = self.bank()
                for kc in range(8):
                    self.mm(psZ[:, 0:nt], Wo_[:, kc, j * 128:(j + 1) * 128], self.Ybf[:, kc, 0:nt], start=(kc == 0), stop=(kc == 7))
                self.stt(xb[:, c8, 0:nt], psZ[:, 0:nt], ga1[:, c8, which:which + 1], xb[:, c8, 0:nt], ALU.mult, ALU.add)
        if bi in (0, 1):
            self.dump("xmid%d_%d" % (l, bi), xb, [128, 8, 512])
        for _ in range(2):
            if self.mod_todo:
                self.mod_piece(*self.mod_todo.pop(0))
        self.rmsnorm_block(xb, nt, self.g2s, sh2, which, self.h2T)
        for u in range(8):
            W1 = self.load_w(self.w_ff1[l][:, u * 512:(u + 1) * 512])
            for j in range(4):
                fc = u * 4 + j
                psH = self.bank()
                for kc in range(8):
                    self.mm(psH[:, 0:nt], W1[:, kc, j * 128:(j + 1) * 128], self.h2T[:, kc, 0:nt], start=(kc == 0), stop=(kc == 7))
                self.act(self.t1[:, 0:nt], psH[:, 0:nt], AF.Relu)
                self.tt("dve", self.Hd[:, fc, 0:nt], psH[:, 0:nt], self.t1[:, 0:nt], ALU.mult)
        for half in range(2):
            psF = [self.bank() for _ in range(4)]
            for kg in range(4):
                W2 = self.load_w(self.w_ff2[l][kg * 1024:(kg + 1) * 1024, half * 512:(half + 1) * 512])
                for j in range(4):
                    for kc in range(8):
                        self.mm(psF[j][:, 0:nt], W2[:, kc, j * 128:(j + 1) * 128], self.Hd[:, kg * 8 + kc, 0:nt],
                                start=(kg == 0 and kc == 0), stop=(kg == 3 and kc == 7))
            for j in range(4):
                c8 = half * 4 + j
                self.stt(xb[:, c8, 0:nt], psF[j][:, 0:nt], ga2[:, c8, which:which + 1], xb[:, c8, 0:nt], ALU.mult, ALU.add)
        if not last:
            self.dma("sp", self.xscr[:, t0:t1].rearrange("(kc p) t -> p kc t", p=128), xb[:, :, 0:nt])
            if self.stop is None:
                modn, g1n, g2n = self.msets[(l + 1) % 2]
                self.rmsnorm_block(xb, nt, g1n, modn[:, 0:8, :], which, self.hT[:, :, t0:t1])
            if bi in (0, 1):
                self.dump("xout%d_%d" % (l, bi), xb, [128, 8, 512])
        else:
            fg = cols[:, O_FG:O_FG + 8].unsqueeze(2)
            ob = self.Vf
            self.rmsnorm_block_f32(xb, nt, fg, ob)
            self.dma("sp", self.outT[:, t0 - CTX:t1 - CTX].rearrange("(kc p) t -> p kc t", p=128), ob[:, :, 0:nt], out_dma=True)

    def rmsnorm_block_f32(self, xb, nt, g, out):
        for kc in range(8):
            self.act(self.sq[:, kc, 0:nt], xb[:, kc, 0:nt], AF.Square)
        ps = self.bank()
        for kc in range(8):
            self.mm(ps[:, 0:nt], self.ones_bf, self.sq[:, kc, 0:nt], start=(kc == 0), stop=(kc == 7))
        self.ts("dve", self.t1[:, 0:nt], ps[:, 0:nt], 1.0 / D, ALU.mult, EPS, ALU.add)
        self.act(self.t2[:, 0:nt], self.t1[:, 0:nt], AF.Sqrt)
        self.recip(self.rstd[:, 0:nt], self.t2[:, 0:nt])
        for kc in range(8):
            self.stt(out[:, kc, 0:nt], xb[:, kc, 0:nt], g[:, kc, 0:1], self.rstd[:, 0:nt], ALU.mult, ALU.mult)


def _colz(v):
    v = np.asarray(v, np.float32)
    return np.ascontiguousarray(v.reshape(-1, 128).T)


def prep_inputs(x, c, ctx, c_ctx, w_mod, b_mod, norm1_g, w_in, mlstm_gate_b, qk_conv_w, conv_dw_w,
                conv_ln_g, conv_ln_b, w_conv_out, mlstm_norm_g, w_mlstm_out, w_out, norm2_g,
                w_ff1, w_ff2, final_g):
    f = lambda a: np.ascontiguousarray(np.asarray(a, np.float32))
    x, c, ctx, c_ctx = f(x), f(c), f(ctx), f(c_ctx)
    consts = np.zeros((128, 640), np.float32)
    idx = np.arange(128)
    consts[idx, idx] = 1.0
    consts[idx, 128 + 127 - idx] = 1.0
    consts[:, 256:384] = (idx[None, :] >= idx[:, None])
    consts[:, 384:512] = (idx[None, :] <= idx[:, None])
    consts[:, 512:640] = 1.0
    shared = dict(consts=consts, w_mod=f(w_mod), w_in=f(w_in), gate_b=f(mlstm_gate_b), mng=f(mlstm_norm_g),
                  w_conv_out=f(w_conv_out), w_m_out=f(w_mlstm_out), w_out=f(w_out), w_ff1=f(w_ff1), w_ff2=f(w_ff2))
    base = np.zeros((128, NCOL), np.float32)
    for l in range(DEPTH):
        cb = l * LC
        base[:, cb + O_BMOD:cb + O_BMOD + 48] = _colz(b_mod[l])
        base[:, cb + O_N1:cb + O_N1 + 8] = _colz(norm1_g[l])
        base[:, cb + O_N2:cb + O_N2 + 8] = _colz(norm2_g[l])
        base[:, cb + O_LNG:cb + O_LNG + 8] = _colz(conv_ln_g[l])
        base[:, cb + O_LNB:cb + O_LNB + 8] = _colz(conv_ln_b[l])
        dw = np.asarray(conv_dw_w[l], np.float32)
        base[:, cb + O_DW:cb + O_DW + 248] = dw.T.reshape(8, 128, 31).transpose(1, 0, 2).reshape(128, 248)
        qk = np.asarray(qk_conv_w[l], np.float32)
        base[:, cb + O_QK:cb + O_QK + 160] = qk.T.reshape(32, 128, 5).transpose(1, 0, 2).reshape(128, 160)
        base[:, cb + O_MNG:cb + O_MNG + 16] = _colz(mlstm_norm_g[l])
    base[:, O_FG:O_FG + 8] = _colz(final_g)
    maps = []
    for b in range(x.shape[0]):
        cols = base.copy()
        cc = np.stack([_colz(c[b]), _colz(c_ctx)], axis=2)
        cols[:, O_C:O_C + 16] = cc.reshape(128, 16)
        xin = np.ascontiguousarray(np.concatenate([ctx[b], x[b]], axis=0).T)
        m = dict(shared)
        m["cols"] = cols
        m["xin"] = xin
        maps.append(m)
    return maps


_NC_CACHE = {}


def kernel(**inputs):
    maps = prep_inputs(**inputs)
    if "nc" not in _NC_CACHE:
        _NC_CACHE["nc"] = Builder().build()
    nc = _NC_CACHE["nc"]
    res = run_bass_kernel_spmd(nc, maps, core_ids=list(range(8)))
    out = np.stack([np.ascontiguousarray(res.results[b]["outT"].T) for b in range(8)], axis=0)
    return out.astype(np.float32)
```

```python
import contextlib
import numpy as np
import concourse.bass as bass
import concourse.mybir as mybir
from concourse.bass_utils import run_bass_kernel_spmd

F32 = mybir.dt.float32
BF = mybir.dt.bfloat16
AF = mybir.ActivationFunctionType
ALU = mybir.AluOpType
F32R = mybir.dt.float32r
DSZ = {F32: 4, BF: 2, F32R: 4}

DEPTH = 2
D = 1024
T = 2304
NT = 18
CTX = 256
SEQ = 2048
HEADS = 4
DH = 512
INW = 12304
EPS = 1e-6
M_INIT = -1e30
BLKS = [(0, 256), (256, 768), (768, 1280), (1280, 1792), (1792, 2304)]
C_GA, C_GB, C_Q, C_K, C_V, C_O, C_G, C_BC, C_BM = 0, 1024, 2048, 4096, 6144, 8192, 10240, 10256, 11280
LC = 504
O_BMOD, O_N1, O_N2, O_LNG, O_LNB, O_DW, O_QK, O_MNG = 0, 48, 56, 64, 72, 80, 328, 488
O_FG = DEPTH * LC
O_C = O_FG + 8
NCOL = O_C + 16

COMPUTE = ("pe", "act", "dve", "pool")


def box_of(ap):
    pat = ap.ap
    off = ap.offset
    name = ap.tensor.name
    sz = DSZ[ap.dtype]
    if "DRAM" in str(ap.space).upper():
        hi = off
        for st, n in pat:
            if st > 0:
                hi += st * (n - 1)
        return (name, 0, 1, off * sz, (hi + 1) * sz)
    if name.startswith("ps"):
        return (name, 0, 128, 0, 2048)
    pstep, pn = pat[0]
    if pstep == 0:
        pstep = 1 << 40
    p0 = off // pstep
    f0 = off % pstep
    f1 = f0
    for st, n in pat[1:]:
        if st > 0:
            f1 += st * (n - 1)
    return (name, p0, p0 + pn, f0 * sz, (f1 + 1) * sz)


def _ovl(a, b):
    return a[1] < b[2] and b[1] < a[2] and a[3] < b[4] and b[3] < a[4]


def _cov(a, b):
    return a[1] <= b[1] and a[2] >= b[2] and a[3] <= b[3] and a[4] >= b[4]


class Op:
    __slots__ = ("eng", "fn", "idx", "waits", "signal", "is_dma", "sem_slot", "target", "vc", "gid")


class Sched:
    def __init__(self, nc, n_dma_sems=24):
        self.nc = nc
        self.ops = {e: [] for e in COMPUTE + ("sp",)}
        self.track = {}
        self.known = {e: {f: -1 for f in COMPUTE} for e in COMPUTE + ("sp",)}
        self.known_dma = {e: set() for e in COMPUTE + ("sp",)}
        self.n_dma_sems = n_dma_sems
        self.dma_slot_last = {}
        self.dma_count = {"sp": 0, "pool": 0}
        self.dma_slot_uses = {}
        self.out_dmas = []
        self.gid = 0

    def _need(self, op, prod):
        e = op.eng
        if prod.is_dma:
            if prod.gid in self.known_dma[e]:
                return
            self.known_dma[e].add(prod.gid)
            op.waits.append(("dma", prod))
            return
        f = prod.eng
        if f == e:
            if e == "pe":
                return
            if self.known[e][e] >= prod.idx:
                return
            self.known[e][e] = prod.idx
            prod.signal = True
            op.waits.append(("eng", prod))
            return
        if self.known[e][f] >= prod.idx:
            return
        prod.signal = True
        op.waits.append(("eng", prod))
        kn = self.known[e]
        kn[f] = prod.idx
        for g, v in prod.vc.items():
            if g != e and kn[g] < v:
                kn[g] = v

    def add(self, eng, fn, reads=(), writes=(), dma=False, out_dma=False):
        op = Op()
        op.eng = eng
        op.fn = fn
        op.idx = len(self.ops[eng])
        op.waits = []
        op.signal = False
        op.is_dma = dma
        op.gid = self.gid
        self.gid += 1
        rb = [box_of(a) for a in reads]
        wb = [box_of(a) for a in writes]
        deps = []
        for b in rb:
            excl = b[0].startswith("ps")
            for (bb, o, k) in self.track.get(b[0], ()):
                if (k == "w" or (excl and o.eng != eng)) and _ovl(b, bb):
                    deps.append(o)
        for b in wb:
            for (bb, o, k) in self.track.get(b[0], ()):
                if _ovl(b, bb):
                    deps.append(o)
        if dma:
            q = eng
            c = self.dma_count[q]
            self.dma_count[q] = c + 1
            slot = c % self.n_dma_sems
            op.sem_slot = (q, slot)
            u = self.dma_slot_uses.get((q, slot), 0) + 1
            self.dma_slot_uses[(q, slot)] = u
            op.target = 16 * u
            prev = self.dma_slot_last.get((q, slot))
            if prev is not None:
                deps.append(prev)
            self.dma_slot_last[(q, slot)] = op
            if out_dma:
                self.out_dmas.append(op)
        best = {}
        for o in deps:
            if o is op:
                continue
            if o.is_dma:
                best[("d", o.gid)] = o
            else:
                cur = best.get(o.eng)
                if cur is None or cur.idx < o.idx:
                    best[o.eng] = o
        for o in best.values():
            self._need(op, o)
        op.vc = dict(self.known[eng])
        if not dma and eng in COMPUTE:
            op.vc[eng] = op.idx - 1
        for b in wb:
            lst = self.track.setdefault(b[0], [])
            lst[:] = [t for t in lst if not _cov(b, t[0])]
            lst.append((b, op, "w"))
        for b in rb:
            lst = self.track.setdefault(b[0], [])
            if not dma:
                lst[:] = [t for t in lst if not (t[2] == "r" and t[1].eng == eng and not t[1].is_dma and _cov(b, t[0]))]
            lst.append((b, op, "r"))
        self.ops[eng].append(op)
        return op

    def emit(self):
        nc = self.nc
        with contextlib.ExitStack() as st:
            esem = {e: st.enter_context(nc.semaphore("s_" + e)) for e in COMPUTE}
            dsem = {}
            for q in ("sp", "pool"):
                for s in range(min(self.n_dma_sems, self.dma_count[q])):
                    dsem[(q, s)] = st.enter_context(nc.semaphore("d_%s_%d" % (q, s)))
            cnt = {}
            for e in COMPUTE:
                c = 0
                for op in self.ops[e]:
                    if op.signal and not op.is_dma:
                        c += 1
                    cnt[op.gid] = c
            block = st.enter_context(nc.Block())

            def run(e, eng):
                for op in self.ops[e]:
                    for kind, prod in op.waits:
                        if kind == "dma":
                            eng.wait_ge(dsem[prod.sem_slot], prod.target)
                        else:
                            eng.wait_ge(esem[prod.eng], cnt[prod.gid])
                    ins = op.fn(eng)
                    if op.is_dma:
                        ins.then_inc(dsem[op.sem_slot], 16)
                    elif op.signal:
                        ins.then_inc(esem[e], 1)
                if e == "sp":
                    for od in self.out_dmas:
                        eng.wait_ge(dsem[od.sem_slot], od.target)

            @block.tensor
            def _(eng):
                run("pe", eng)

            @block.scalar
            def _(eng):
                run("act", eng)

            @block.vector
            def _(eng):
                run("dve", eng)

            @block.gpsimd
            def _(eng):
                run("pool", eng)

            @block.sync
            def _(eng):
                run("sp", eng)


ARENA_BYTES = 212800


class Builder:
    def __init__(self, n_layers=DEPTH, dbg=(), stop=None):
        self.n_layers = n_layers
        self.dbg_req = dict(dbg)
        self.stop = stop
        self.nc = bass.Bass("TRN2", target_bir_lowering=False)
        self.bank_i = 0
        self.ring_i = 0
        self.flip = 0

    def mm(self, out, lhsT, rhs, start=True, stop=True):
        self.S.add("pe", lambda e: e.matmul(out, lhsT=lhsT, rhs=rhs, start=start, stop=stop),
                   reads=[lhsT, rhs], writes=[out])

    def tr(self, out, in_, ident):
        self.S.add("pe", lambda e: e.transpose(out, in_, ident), reads=[in_, ident], writes=[out])

    def act(self, out, in_, func, bias=None, scale=None, accum_out=None):
        kw = {}
        rd = [in_]
        wr = [out]
        if bias is not None:
            kw["bias"] = bias
            if not isinstance(bias, (int, float)):
                rd.append(bias)
        if scale is not None:
            kw["scale"] = scale
            if not isinstance(scale, (int, float)):
                rd.append(scale)
        if accum_out is not None:
            kw["accum_out"] = accum_out
            wr.append(accum_out)
        self.S.add("act", lambda e: e.activation(out=out, in_=in_, func=func, **kw), reads=rd, writes=wr)

    def tt(self, eng, out, in0, in1, op):
        self.S.add(eng, lambda e: e.tensor_tensor(out=out, in0=in0, in1=in1, op=op), reads=[in0, in1], writes=[out])

    def ts(self, eng, out, in0, s1, op0, s2=None, op1=None):
        rd = [in0]
        if not isinstance(s1, (int, float)):
            rd.append(s1)
        if s2 is not None and not isinstance(s2, (int, float)):
            rd.append(s2)
        if op1 is None:
            self.S.add(eng, lambda e: e.tensor_scalar(out=out, in0=in0, scalar1=s1, scalar2=None, op0=op0),
                       reads=rd, writes=[out])
        else:
            self.S.add(eng, lambda e: e.tensor_scalar(out=out, in0=in0, scalar1=s1, scalar2=s2, op0=op0, op1=op1),
                       reads=rd, writes=[out])

    def stt(self, out, in0, scalar, in1, op0, op1):
        rd = [in0, in1]
        if not isinstance(scalar, (int, float)):
            rd.append(scalar)
        self.S.add("dve", lambda e: e.scalar_tensor_tensor(out=out, in0=in0, scalar=scalar, in1=in1, op0=op0, op1=op1),
                   reads=rd, writes=[out])

    def copy(self, eng, out, in_):
        if eng == "act":
            self.S.add("act", lambda e: e.copy(out=out, in_=in_), reads=[in_], writes=[out])
        else:
            self.S.add(eng, lambda e: e.tensor_copy(out=out, in_=in_), reads=[in_], writes=[out])

    def recip(self, out, in_):
        self.S.add("dve", lambda e: e.reciprocal(out=out, in_=in_), reads=[in_], writes=[out])

    def memset(self, eng, out, val):
        self.S.add(eng, lambda e: e.memset(out, val), reads=[], writes=[out])

    def scan(self, out, d0, d1, init, op0, op1):
        self.S.add("dve", lambda e: e.tensor_tensor_scan(out=out, data0=d0, data1=d1, initial=init, op0=op0, op1=op1),
                   reads=[d0, d1], writes=[out])

    def dma(self, q, out, in_, out_dma=False):
        self.S.add(q, lambda e: e.dma_start(out=out, in_=in_), reads=[in_], writes=[out], dma=True, out_dma=out_dma)

    def alt(self):
        self.flip ^= 1
        return "act" if self.flip else "dve"

    def bank(self):
        b = self.psb[self.bank_i % 8]
        self.bank_i += 1
        return b[:]

    def view(self, off, shape, dt, parts=128):
        n = 1
        for s in shape:
            n *= s
        nb = n * DSZ[dt]
        assert off % 4 == 0 and off + nb <= ARENA_BYTES, (off, nb)
        a = self.arena[0:parts, off // 4:(off + nb + 3) // 4]
        if dt == BF:
            a = a.bitcast(BF)
            if a.shape[1] != n:
                a = a[:, 0:n]
        if len(shape) == 2:
            a = a.rearrange("p (a b) -> p a b", a=shape[0])
        elif len(shape) == 3:
            a = a.rearrange("p (a b c) -> p a b c", a=shape[0], b=shape[1])
        return a

    def load_w(self, src, key=None):
        pre = getattr(self, "preloaded", None)
        if key is not None and pre and key in pre:
            return pre.pop(key)
        slot = self.ring[self.ring_i % len(self.ring)]
        self.ring_i += 1
        self.dma("pool", slot, src.rearrange("(kc p) n -> p kc n", p=128))
        return slot

    def dump(self, name, ap, shape):
        if name not in self.dbg_req:
            return
        dt = ap.dtype
        d = self.nc.dram_tensor("dbg_" + name, list(shape), dt, kind="ExternalOutput").ap()
        self.dma("sp", d, ap, out_dma=True)
        self.dbg_out.append("dbg_" + name)

    def build(self):
        nc = self.nc
        L = DEPTH
        self.dbg_out = []
        dr = lambda n, s, dt=F32, k="ExternalInput": nc.dram_tensor(n, list(s), dt, kind=k).ap()
        self.xin = dr("xin", [D, T])
        self.cols_d = dr("cols", [128, NCOL])
        self.consts_d = dr("consts", [128, 5 * 128])
        self.w_mod = dr("w_mod", [L, D, 6 * D])
        self.w_in = dr("w_in", [L, D, INW])
        self.gate_b = dr("gate_b", [L, 16])
        self.mng = dr("mng", [L, 2 * D])
        self.w_conv_out = dr("w_conv_out", [L, D, D])
        self.w_m_out = dr("w_m_out", [L, 2 * D, D])
        self.w_out = dr("w_out", [L, D, D])
        self.w_ff1 = dr("w_ff1", [L, D, 4 * D])
        self.w_ff2 = dr("w_ff2", [L, 4 * D, D])
        self.outT = dr("outT", [D, SEQ], F32, "ExternalOutput")
        self.xscr = dr("xscr", [D, T], F32, "Internal")
        self.hmscr = dr("hmscr", [2 * D, T], BF, "Internal")

        with contextlib.ExitStack() as st:
            self.arena = st.enter_context(nc.sbuf_tensor("arena", [128, ARENA_BYTES // 4], F32))
            self.psb = [st.enter_context(nc.psum_tensor("ps%d" % i, [128, 512], F32)) for i in range(8)]
            self.S = Sched(nc)
            self.layout()
            self.setup()
            for l in range(self.n_layers):
                self.layer(l)
                if self.stop is not None and self.stop[0] == l:
                    break
            self.S.emit()
        return nc

    def layout(self):
        o = 0

        def take(nbytes):
            nonlocal o
            r = o
            o += (nbytes + 3) // 4 * 4
            return r

        V = self.view
        self.hT = V(take(8 * T * 2), (8, T), BF)
        self.ring = [V(take(8192), (8, 512), BF) for _ in range(3)]
        self.cols = V(take(NCOL * 4), (NCOL,), F32)
        cf = take(5 * 512)
        self.ident_f = V(cf, (128,), F32)
        self.J_f = V(cf + 512, (128,), F32)
        self.mask = [V(cf + 1024, (128,), F32), V(cf + 1536, (128,), F32)]
        self.ones_f = V(cf + 2048, (128,), F32)
        self.ident_bf = V(take(256), (128,), BF)
        self.ones_bf = V(take(256), (128,), BF)
        self.negh = V(take(4), (1,), F32)
        self.ones_r = V(take(512), (128,), F32).bitcast(F32R)
        self.msets = [(V(take(96 * 4), (48, 2), F32), V(take(64), (8, 2), F32), V(take(64), (8, 2), F32)) for _ in range(2)]
        self.sT = V(take(32), (8, 2), BF)
        self.stmp = V(take(64), (8, 2), F32)
        self.gb_bc = V(take(64), (16,), F32)
        self.G = V(take(NT * 16 * 4), (NT, 16), F32)
        self.Wg = V(take(8 * 16 * 2), (8, 16), BF)
        self.WS = [V(take(288), (72,), F32) for _ in range(2)]
        self.EBA = [V(take(288), (72,), F32) for _ in range(2)]
        self.WE = [V(take(288), (72,), F32) for _ in range(2)]
        self.base = o
        o = self.base
        self.qT = V(take(4 * T * 2), (4, T), BF)
        self.kT = V(take(4 * T * 2), (4, T), BF)
        self.v = V(take(NT * 512 * 2), (NT, 512), BF)
        hp = take(NT * 512 * 2)
        self.hpart = V(hp, (NT, 512), BF)
        self.raw = V(hp, (4, T), BF)
        self.C = [[V(take(4 * 512 * 4), (4, 512), F32) for _ in range(2)] for _ in range(2)]
        csb = take(4 * 4096)
        self.Csb = [[V(csb + 4096 * (2 * i + j), (4, 512), BF) for j in range(2)] for i in range(2)]
        self.dg5 = V(csb, (20, 128), BF)
        self.nst = [[V(take(16), (4,), F32) for _ in range(2)] for _ in range(2)]
        self.nsb = [[V(take(8), (4,), BF) for _ in range(2)] for _ in range(2)]
        self.kw = [[V(take(1024), (512,), BF) for _ in range(2)] for _ in range(2)]
        self.PT = [[V(take(256), (128,), BF) for _ in range(3)] for _ in range(2)]
        self.dd = [V(take(4), (1,), F32) for _ in range(2)]
        self.rr = [V(take(4), (1,), F32) for _ in range(2)]
        self.hsum = V(take(2048), (512,), F32)
        self.ssq = V(take(4), (1,), F32)
        self.rs1 = V(take(4), (1,), F32)
        self.rs2 = V(take(4), (1,), F32)
        sg_o = take(2048)
        self.sg = V(sg_o, (512,), F32)
        self.hmgs = [V(take(1024), (512,), BF) for _ in range(2)]
        self.junk = V(take(1024), (512,), BF)
        self.hst = [V(take(1024), (4, 128), BF) for _ in range(2)]
        self.sgt = V(sg_o, (512,), F32)
        self.gset = []
        o_save = o
        o = csb
        for _d in range(2):
            g = {}
            g["ct"] = [V(take(512), (128,), F32, parts=72) for _ in range(8)]
            g["icont"] = V(take(288), (72,), F32)
            g["fcont"] = V(take(288), (72,), F32)
            g["rows"] = V(take(3 * 288), (3, 72), F32, parts=1)
            g["mtmp"] = V(take(16), (4,), F32, parts=1)
            g["c72"] = [V(take(4), (1,), F32, parts=72) for _ in range(5)]
            g["dgw"] = V(take(288), (72,), F32, parts=72)
            g["t128"] = V(take(288), (72,), F32)
            self.gset.append(g)
            self.gset.append(g)
            break
        assert o <= csb + 8192
        o = o_save
        self.scan_end = o
        assert o <= ARENA_BYTES, o
        o = self.base
        self.xb = V(take(8 * 512 * 4), (8, 512), F32)
        sq_o = take(8 * 512 * 4)
        self.sq = V(sq_o, (8, 512), BF)
        self.Vb = V(sq_o + 8192, (8, 512), BF)
        self.Vf = V(take(8 * 512 * 4), (8, 512), F32)
        self.rstd = V(take(2048), (512,), F32)
        self.t1 = V(take(2048), (512,), F32)
        self.t2 = V(take(2048), (512,), F32)
        self.t3 = V(take(2048), (512,), F32)
        ub = take(8 * 656 * 2)
        self.U = V(ub, (8, 656), BF)
        zb = take(8 * 512 * 2)
        self.Z = V(zb, (8, 512), BF)
        self.Ybf = V(ub, (8, 512), BF)
        self.Y = V(take(8 * 512 * 4), (8, 512), F32)
        self.Hd = V(ub, (32, 512), BF)
        hb = take(16 * 512 * 2)
        self.dg31 = [V(hb + 7936 * i, (31, 128), BF) for i in range(2)]
        self.S8 = V(hb, (8, 512), BF)
        self.S7 = V(hb + 8192, (8, 512), BF)
        self.HM = V(self.base + 16384, (16, 512), BF)
        self.h2T = V(take(8 * 512 * 2), (8, 512), BF)
        self.cacc = V(take(656 * 4), (656,), F32)
        self.ring_extra = [V(take(8192), (8, 512), BF) for _ in range(2)]
        assert o <= ARENA_BYTES, o
        self.final_end = o

    def setup(self):
        self.dma("sp", self.cols, self.cols_d)
        cf = self.consts_d
        self.dma("sp", self.ident_f, cf[:, 0:128])
        self.dma("sp", self.J_f, cf[:, 128:256])
        self.dma("sp", self.mask[0], cf[:, 256:384])
        self.dma("sp", self.mask[1], cf[:, 384:512])
        self.dma("sp", self.ones_f, cf[:, 512:640])
        self.copy("dve", self.ident_bf, self.ident_f)
        self.copy("dve", self.ones_bf, self.ones_f)
        self.memset("dve", self.negh, -0.5)

    def rmsnorm_block(self, xb, nt, gs, sh, which, out):
        for kc in range(8):
            self.act(self.sq[:, kc, 0:nt], xb[:, kc, 0:nt], AF.Square)
        ps = self.bank()
        for kc in range(8):
            self.mm(ps[:, 0:nt], self.ones_bf, self.sq[:, kc, 0:nt], start=(kc == 0), stop=(kc == 7))
        self.ts("dve", self.t1[:, 0:nt], ps[:, 0:nt], 1.0 / D, ALU.mult, EPS, ALU.add)
        self.act(self.t2[:, 0:nt], self.t1[:, 0:nt], AF.Sqrt)
        self.recip(self.rstd[:, 0:nt], self.t2[:, 0:nt])
        for kc in range(8):
            tmp = (self.t3, self.t1, self.t2)[kc % 3]
            self.stt(tmp[:, 0:nt], xb[:, kc, 0:nt], gs[:, kc, which:which + 1], self.rstd[:, 0:nt], ALU.mult, ALU.mult)
            if sh is None:
                self.copy("act", out[:, kc, 0:nt], tmp[:, 0:nt])
            else:
                self.act(out[:, kc, 0:nt], tmp[:, 0:nt], AF.Identity, bias=sh[:, kc, which:which + 1], scale=1.0)

    def mod_piece(self, l, u):
        cols = self.cols
        cb = l * LC
        mod, g1s, g2s = self.msets[l % 2]
        if u == 0:
            cc = cols[:, O_C:O_C + 16].rearrange("p (a b) -> p a b", a=8)
            self.act(self.stmp, cc, AF.Sigmoid)
            self.tt("dve", self.sT, self.stmp, cc, ALU.mult)
        psm = self.bank()
        W = self.load_w(self.w_mod[l][:, u * 512:(u + 1) * 512])
        for j in range(4):
            for kc in range(8):
                self.mm(psm[:, j * 2:j * 2 + 2], W[:, kc, j * 128:(j + 1) * 128], self.sT[:, kc, :],
                        start=(kc == 0), stop=(kc == 7))
        bm = cols[:, cb + O_BMOD + u * 4:cb + O_BMOD + u * 4 + 4]
        self.tt("dve", mod[:, u * 4:(u + 1) * 4, :], psm[:, 0:8].rearrange("p (a b) -> p a b", b=2),
                bm.unsqueeze(2).to_broadcast([128, 4, 2]), ALU.add)
        if u == 3:
            sc1 = mod[:, 8:16, :]
            n1 = cols[:, cb + O_N1:cb + O_N1 + 8].unsqueeze(2).to_broadcast([128, 8, 2])
            self.ts("dve", g1s, sc1, 1.0, ALU.add)
            self.tt("dve", g1s, g1s, n1, ALU.mult)
        if u == 9:
            sc2 = mod[:, 32:40, :]
            n2 = cols[:, cb + O_N2:cb + O_N2 + 8].unsqueeze(2).to_broadcast([128, 8, 2])
            self.ts("dve", g2s, sc2, 1.0, ALU.add)
            self.tt("dve", g2s, g2s, n2, ALU.mult)
        if u == 11:
            self.dump("mod%d" % l, mod, [128, 48, 2])

    def modulation(self, l):
        for u in range(12):
            self.mod_piece(l, u)

    def layer(self, l):
        last = (l == DEPTH - 1)
        cb = l * LC
        cols = self.cols
        xsrc = self.xin if l == 0 else self.xscr
        fused = self.stop is None
        if l == 0:
            for u in range(4):
                self.mod_piece(0, u)
            self.mod_heads = [(0, u) for u in range(4, 12)]
        else:
            self.mod_heads = [] if fused else []
        if l + 1 < self.n_layers and fused:
            self.mod_next_early = [(l + 1, u) for u in range(4)]
            self.mod_todo = [(l + 1, u) for u in range(4, 12)]
        else:
            self.mod_next_early = []
            self.mod_todo = []
        if l > 0 and not fused:
            self.modulation(l)
        mod, g1s, g2s = self.msets[l % 2]
        self.g1s, self.g2s = g1s, g2s
        sh1, sc1, ga1, sh2, sc2, ga2 = [mod[:, m * 8:(m + 1) * 8, :] for m in range(6)]

        if l == 0 or not fused:
            for bi, (t0, t1) in enumerate(BLKS):
                nt = t1 - t0
                which = 1 if bi == 0 else 0
                xb = (self.xb, self.Vf)[bi % 2]
                self.dma("sp", xb[:, :, 0:nt], xsrc[:, t0:t1].rearrange("(kc p) t -> p kc t", p=128))
                self.rmsnorm_block(xb, nt, self.g1s, sh1, which, self.hT[:, :, t0:t1])
        self.dump("hT%d" % l, self.hT, [128, 8, T])
        if self.stop == (l, "B"):
            return

        self.gates(l)
        if self.stop == (l, "C"):
            self.gates2(l)
            return

        for h in range(HEADS):
            self.head(l, h, last)
            if self.stop == (l, "D%d" % h):
                return

        ring3 = self.ring
        self.ring = ring3 + self.ring_extra
        for bi, (t0, t1) in enumerate(BLKS):
            if last and bi == 0:
                continue
            self.final_block(l, bi, t0, t1, last, sh1, sc1, ga1, sh2, sc2, ga2)
            if self.stop == (l, "E%d" % bi):
                return
        self.ring = ring3

    def gates(self, l):
        self.dma("pool", self.Wg, self.w_in[l][:, C_G:C_G + 16].rearrange("(kc p) n -> p kc n", p=128))
        self.dma("sp", self.gb_bc, self.gate_b[l].partition_broadcast(128))
        psg = self.bank()
        for tl in range(NT):
            for kc in range(8):
                self.mm(psg[:, tl * 16:(tl + 1) * 16], self.hT[:, kc, tl * 128:(tl + 1) * 128], self.Wg[:, kc, :],
                        start=(kc == 0), stop=(kc == 7))
        self.tt("dve", self.G, psg[:, 0:NT * 16].rearrange("p (a b) -> p a b", b=16),
                self.gb_bc.unsqueeze(1).to_broadcast([128, NT, 16]), ALU.add)
        self.dump("G%d" % l, self.G, [128, NT, 16])
        self.pending_gates = l

    def gates2(self, l):
        import os
        gs0 = gs = float(os.environ.get("GSTOP", "99"))
        if gs <= 1:
            return
        id72 = self.ident_f[0:72, 0:72]
        gdirs = [int(x) for x in os.environ.get("GDIRS", "0,1").split(",")]
        for gi, d in enumerate(gdirs):
            R = self.ident_f if d == 0 else self.J_f
            gs = gs0 if gi == len(gdirs) - 1 else 99
            self.bank_i += int(os.environ.get("GBUMP", "0")) if gi > 0 else 0
            gq = self.gset[gi % 2]
            self.ct, self.icont, self.fcont, self.rows = gq["ct"], gq["icont"], gq["fcont"], gq["rows"]
            self.mtmp, self.c72, self.dgw, self.t128 = gq["mtmp"], gq["c72"], gq["dgw"], gq["t128"]
            cti, ctf, sp, cs, a, cm, wsc, ebc = self.ct
            ic3 = self.icont.rearrange("p (a b) -> p a b", b=4)
            fc3 = self.fcont.rearrange("p (a b) -> p a b", b=4)
            self.copy("dve", ic3, self.G[:, :, d * 8:d * 8 + 4])
            self.copy("dve", fc3, self.G[:, :, d * 8 + 4:d * 8 + 8])
            if gs <= 1.2:
                return
            ps = self.bank()
            self.mm(ps[0:72, 0:128], self.icont, R)
            self.mm(ps[0:72, 128:256], self.fcont, R)
            if gs <= 1.5:
                return
            self.copy("dve", cti, ps[0:72, 0:128])
            if gs <= 1.7:
                return
            zc = self.c72[4]
            self.memset("dve", zc, 0.0)
            self.copy("dve", ctf, ps[0:72, 128:256])
            self.act(sp, ctf, AF.Exp, bias=zc, scale=-1.0)
            if gs <= 1.8:
                return
            self.act(sp, sp, AF.Ln, bias=1.0)
            if gs <= 2:
                return
            self.scan(cs, self.ones_f[0:72, :], sp, 0.0, ALU.mult, ALU.add)
            self.tt("dve", a, cti, cs, ALU.add)
            self.scan(cm, a, a, M_INIT, ALU.max, ALU.max)
            if gs <= 3:
                return
            ps2 = self.bank()
            self.mm(ps2[0:1, 0:72], cm[:, 127:128], id72)
            self.mm(ps2[0:1, 72:144], cs[:, 127:128], id72)
            rows = self.rows
            self.copy("dve", rows[:, 0, :], ps2[0:1, 0:72])
            self.copy("dve", rows[:, 1, :], ps2[0:1, 72:144])
            order = list(range(NT)) if d == 0 else [1, 0] + list(range(NT - 1, 1, -1))
            mrow = rows[:, 2, :]
            self.memset("dve", mrow[:, order[0] * 4:order[0] * 4 + 4], M_INIT)
            for i in range(NT - 1):
                c, n = order[i], order[i + 1]
                self.tt("dve", self.mtmp, mrow[:, c * 4:c * 4 + 4], rows[:, 0, c * 4:c * 4 + 4], ALU.max)
                self.tt("dve", mrow[:, n * 4:n * 4 + 4], self.mtmp, rows[:, 1, c * 4:c * 4 + 4], ALU.subtract)
            if gs <= 4:
                return
            ps3 = self.bank()
            self.mm(ps3[0:72, 0:1], mrow, self.ones_f[0:1, 0:1])
            mst, A, negA, wec, _ = self.c72
            self.copy("dve", mst, ps3[0:72, 0:1])
            self.tt("dve", A, mst, cm[:, 127:128], ALU.max)
            self.ts("dve", negA, A, -1.0, ALU.mult)
            self.act(wsc, a, AF.Exp, bias=negA, scale=1.0)
            self.act(ebc, cs, AF.Exp, bias=negA, scale=1.0)
            self.act(wec, mst, AF.Exp, bias=negA, scale=1.0)
            if gs <= 5:
                return
            for src, dst in ((wsc, self.WS[d]), (ebc, self.EBA[d])):
                ps4 = self.bank()
                self.mm(ps4[:, 0:72], src, id72)
                if d == 0:
                    self.copy("dve", dst, ps4[:, 0:72])
                else:
                    self.copy("dve", self.t128, ps4[:, 0:72])
                    ps5 = self.bank()
                    self.mm(ps5[:, 0:72], self.J_f, self.t128)
                    self.copy("dve", dst, ps5[:, 0:72])
            if gs <= 6:
                return
            self.ts("dve", self.dgw, id72, wec, ALU.mult)
            ps6 = self.bank()
            self.mm(ps6[:, 0:72], self.ones_f[0:72, :], self.dgw)
            self.copy("dve", self.WE[d], ps6[:, 0:72])
            if gs <= 7:
                return
            self.dump("WS%d_%d" % (l, d), self.WS[d], [128, 72])
            self.dump("EBA%d_%d" % (l, d), self.EBA[d], [128, 72])
            self.dump("WE%d_%d" % (l, d), self.WE[d], [128, 72])

    def head(self, l, h, last):
        cols = self.cols
        cb = l * LC
        for wi, (c0, dst, scl) in enumerate(((C_Q, self.qT, 1.0), (C_K, self.kT, DH ** -0.5))):
            W = self.load_w(self.w_in[l][:, c0 + h * 512:c0 + (h + 1) * 512], key=(l, h, wi))
            for dc in range(4):
                ch = (0 if wi == 0 else 16) + h * 4 + dc
                for o in range(5):
                    wc = cols[:, cb + O_QK + ch * 5 + o:cb + O_QK + ch * 5 + o + 1]
                    self.ts("pool", self.dg5[:, dc * 5 + o, :], self.ident_bf, wc, ALU.mult, 0.0, ALU.add)
            for (t0, t1) in BLKS:
                nt = t1 - t0
                for dc in range(4):
                    ps = self.bank()
                    for kc in range(8):
                        self.mm(ps[:, 0:nt], W[:, kc, dc * 128:(dc + 1) * 128], self.hT[:, kc, t0:t1],
                                start=(kc == 0), stop=(kc == 7))
                    self.copy(self.alt(), self.raw[:, dc, t0:t1], ps[:, 0:nt])
            for bi, (t0, t1) in enumerate(BLKS):
                nt = t1 - t0
                s0, s1 = (0, CTX) if bi == 0 else (CTX, T)
                for dc in range(4):
                    ps = self.bank()
                    taps = [0, -2, -1, 1, 2]
                    for ti, o in enumerate(taps):
                        jl = max(t0, s0 - o)
                        jh = min(t1, s1 - o)
                        self.mm(ps[:, jl - t0:jh - t0], self.dg5[:, dc * 5 + (o + 2), :], self.raw[:, dc, jl + o:jh + o],
                                start=(ti == 0), stop=(ti == 4))
                    self.act(self.sgt[:, 0:nt], ps[:, 0:nt], AF.Sigmoid)
                    self.stt(dst[:, dc, t0:t1], ps[:, 0:nt], scl, self.sgt[:, 0:nt], ALU.mult, ALU.mult)
        if h == 0:
            self.dump("qT%d" % l, self.qT, [128, 4, T])
            self.dump("kT%d" % l, self.kT, [128, 4, T])
        W = self.load_w(self.w_in[l][:, C_V + h * 512:C_V + (h + 1) * 512])
        for tl in range(NT):
            ps = self.bank()
            for kc in range(8):
                self.mm(ps, self.hT[:, kc, tl * 128:(tl + 1) * 128], W[:, kc, :], start=(kc == 0), stop=(kc == 7))
            self.copy(self.alt(), self.v[:, tl, :], ps)
        if h == 0:
            self.dump("v%d" % l, self.v, [128, NT, 512])
        for _ in range(2):
            if self.mod_heads:
                self.mod_piece(*self.mod_heads.pop(0))
        if self.mod_next_early:
            self.mod_piece(*self.mod_next_early.pop(0))
        if getattr(self, "pending_gates", None) is not None:
            self.gates2(self.pending_gates)
            self.pending_gates = None
        Wo = self.load_w(self.w_in[l][:, C_O + h * 512:C_O + (h + 1) * 512])
        if h + 1 == HEADS:
            fb = 1 if last else 0
            self.preloaded = {}
            self.preloaded[("ga", l, fb, 0)] = self.load_w(self.w_in[l][:, C_GA:C_GA + 512])
            self.preloaded[("gb", l, fb, 0)] = self.load_w(self.w_in[l][:, C_GB:C_GB + 512])
        if h + 1 < HEADS:
            self.preloaded = {}
            for wi, c0 in enumerate((C_Q, C_K)):
                self.preloaded[(l, h + 1, wi)] = self.load_w(self.w_in[l][:, c0 + (h + 1) * 512:c0 + (h + 2) * 512])
        orders = [list(range(NT)), [1, 0] + list(range(NT - 1, 1, -1))]
        visited = set()
        self.hst_i = 0
        self.pending = []
        qT, kT, v = self.qT, self.kT, self.v
        need = lambda tl: not (last and tl < 2)
        psK = {}
        psn = {}

        def cols_of(d, i):
            tl = orders[d][i]
            col = tl * 4 + h
            return (tl, slice(tl * 128, (tl + 1) * 128), self.WS[d][:, col:col + 1], self.EBA[d][:, col:col + 1],
                    self.WE[d][:, col:col + 1])

        def stA(i):
            if i >= NT - 1:
                return
            for d in range(2):
                tl, tok, ws, eba, we = cols_of(d, i)
                psT = self.bank()
                psTb = psT.bitcast(BF)
                for dc in range(4):
                    self.tr(psTb[:, dc * 128:(dc + 1) * 128], kT[:, dc, tok], self.ident_bf)
                self.act(self.kw[d][i % 2], psTb[:, 0:512], AF.Copy, scale=ws)

        def stB(i):
            for d in range(2):
                tl, tok, ws, eba, we = cols_of(d, i)
                if not need(tl):
                    continue
                psS = self.bank()
                for dc in range(4):
                    self.mm(psS[:, 0:128], kT[:, dc, tok], qT[:, dc, tok], start=(dc == 0), stop=(dc == 3))
                self.stt(self.PT[d][i % 3], psS[:, 0:128], ws, self.mask[d], ALU.mult, ALU.mult)

        def stC(i, d):
            if True:
                tl, tok, ws, eba, we = cols_of(d, i)
                r = i % 2
                if i < NT - 1:
                    kw = self.kw[d][i % 2]
                    for dc in range(4):
                        psK[(d, dc)] = self.bank()
                        self.mm(psK[(d, dc)], kw[:, dc * 128:(dc + 1) * 128], v[:, tl, :])
                    psn[d] = self.bank()
                    for dc in range(4):
                        self.mm(psn[d][:, dc:dc + 1], kw[:, dc * 128:(dc + 1) * 128], self.ones_bf[:, 0:1])
                if i > 0 and need(tl):
                    C = self.C[d][r]
                    self.act(self.Csb[d][r][:, 0:2, :], C[:, 0:2, :], AF.Copy, scale=we)
                    self.ts("pool", self.Csb[d][r][:, 2:4, :], C[:, 2:4, :], we, ALU.mult, 0.0, ALU.add)
                    self.ts("pool", self.nsb[d][r], self.nst[d][r], we, ALU.mult, 0.0, ALU.add)

        def stD(i, d):
            if i >= NT - 1:
                return
            if True:
                tl, tok, ws, eba, we = cols_of(d, i)
                r, w = i % 2, 1 - i % 2
                for dc in range(4):
                    if i == 0:
                        self.copy("dve", self.C[d][w][:, dc, :], psK[(d, dc)])
                    else:
                        self.stt(self.C[d][w][:, dc, :], self.C[d][r][:, dc, :], we, psK[(d, dc)], ALU.mult, ALU.add)
                if i == 0:
                    self.copy("dve", self.nst[d][w], psn[d][:, 0:4])
                else:
                    self.stt(self.nst[d][w], self.nst[d][r], we, psn[d][:, 0:4], ALU.mult, ALU.add)

        def stE(i):
            self.out_stage2()
            for d in range(2):
                tl, tok, ws, eba, we = cols_of(d, i)
                if need(tl):
                    PT = self.PT[d][i % 3]
                    Csb, nsb = self.Csb[d][i % 2], self.nsb[d][i % 2]
                    psN = self.bank()
                    psD = self.bank()
                    if i > 0:
                        for dc in range(4):
                            self.mm(psN, qT[:, dc, tok], Csb[:, dc, :], start=(dc == 0), stop=False)
                    self.mm(psN, PT, v[:, tl, :], start=(i == 0), stop=True)
                    if i > 0:
                        for dc in range(4):
                            self.mm(psD[:, 0:1], qT[:, dc, tok], nsb[:, dc:dc + 1], start=(dc == 0), stop=False)
                    self.mm(psD[:, 0:1], PT, self.ones_bf[:, 0:1], start=(i == 0), stop=True)
                    self.ts("dve", self.dd[d], psD[:, 0:1], -1.0, ALU.mult)
                    self.tt("dve", self.dd[d], psD[:, 0:1], self.dd[d], ALU.max)
                    self.tt("dve", self.dd[d], self.dd[d], eba, ALU.max)
                    self.recip(self.rr[d], self.dd[d])
                    if tl not in visited:
                        self.act(self.hpart[:, tl, :], psN, AF.Copy, scale=self.rr[d])
                        if h == 0 and i == 0 and d == 0:
                            self.dump("he%d" % l, self.hpart[:, tl, :], [128, 512])
                            self.dump("rr%d" % l, self.rr[d], [128, 1])
                    else:
                        self.stt(self.hsum, psN, self.rr[d], self.hpart[:, tl, :], ALU.mult, ALU.add)
                        self.out_stage(l, h, tl, Wo)
                visited.add(tl)

        stA(0)
        stB(0)
        if h == 0:
            self.dump("pt%d" % l, self.PT[0][0], [128, 128])
        for i in range(NT):
            for d in range(2):
                stC(i, d)
                stD(i, d)
            if i + 1 < NT:
                stA(i + 1)
                stB(i + 1)
            if i >= 1:
                stE(i - 1)
        stE(NT - 1)
        self.out_stage2()
        if h == 0:
            self.dump("hp%d" % l, self.hpart, [128, NT, 512])

    def out_stage(self, l, h, tl, Wo):
        tok = slice(tl * 128, (tl + 1) * 128)
        self.act(self.junk, self.hsum, AF.Square, accum_out=self.ssq)
        self.ts("dve", self.rs1, self.ssq, 1.0 / DH, ALU.mult, EPS, ALU.add)
        self.tt("pool", self.rs2, self.rs1, self.negh, ALU.pow)
        psO = self.bank()
        for kc in range(8):
            self.mm(psO, self.hT[:, kc, tok], Wo[:, kc, :], start=(kc == 0), stop=(kc == 7))
        self.act(self.sg, psO, AF.Sigmoid)
        hmg = self.hmgs[self.hst_i % 2]
        self.stt(hmg, self.hsum, self.rs2, self.sg, ALU.mult, ALU.mult)
        if h == 0 and tl in (0, 5):
            self.dump("hsum%d_%d" % (l, tl), self.hsum, [128, 512])
        self.pending.append((l, h, tl, hmg, self.hst[self.hst_i % 2]))
        self.hst_i += 1

    def out_stage2(self):
        while self.pending:
            l, h, tl, hmg, hst = self.pending.pop(0)
            tok = slice(tl * 128, (tl + 1) * 128)
            psH = self.bank()
            psHb = psH.bitcast(BF)
            for ec in range(4):
                self.tr(psHb[:, ec * 128:(ec + 1) * 128], hmg[:, ec * 128:(ec + 1) * 128], self.ident_bf)
            cb = l * LC
            for ec in range(4):
                g = self.cols[:, cb + O_MNG + h * 4 + ec:cb + O_MNG + h * 4 + ec + 1]
                self.act(hst[:, ec, :], psHb[:, ec * 128:(ec + 1) * 128], AF.Copy, scale=g)
            if h == 0 and tl in (0, 5):
                self.dump("hmT%d_%d" % (l, tl), hst, [128, 4, 128])
            dst = self.hmscr[h * 512:(h + 1) * 512, tok].rearrange("(ec p) t -> p ec t", p=128)
            self.dma("sp", dst, hst)

    def final_block(self, l, bi, t0, t1, last, sh1, sc1, ga1, sh2, sc2, ga2):
        nt = t1 - t0
        which = 1 if bi == 0 else 0
        cols = self.cols
        cb = l * LC
        xsrc = self.xin if l == 0 else self.xscr
        win = self.w_in[l]
        hTb = self.hT[:, :, t0:t1]
        xb = self.xb
        self.dma("sp", xb[:, :, 0:nt], xsrc[:, t0:t1].rearrange("(kc p) t -> p kc t", p=128))
        rl = 256 if bi == 0 else 64
        nr = nt // rl
        stride = rl + 16
        rpg = 1 if bi == 0 else 4
        gw = rpg * stride
        self.memset("dve", self.U, 0.0)
        for half in range(2):
            Wa = self.load_w(win[:, C_GA + half * 512:C_GA + (half + 1) * 512], key=("ga", l, bi, half))
            Wb = self.load_w(win[:, C_GB + half * 512:C_GB + (half + 1) * 512], key=("gb", l, bi, half))
            for j in range(4):
                c8 = half * 4 + j
                psA = self.bank()
                psB = self.bank()
                for kc in range(8):
                    self.mm(psA[:, 0:nt], Wa[:, kc, j * 128:(j + 1) * 128], hTb[:, kc, :], start=(kc == 0), stop=(kc == 7))
                for kc in range(8):
                    self.mm(psB[:, 0:nt], Wb[:, kc, j * 128:(j + 1) * 128], hTb[:, kc, :], start=(kc == 0), stop=(kc == 7))
                self.act(self.t1[:, 0:nt], psB[:, 0:nt], AF.Sigmoid)
                self.tt("dve", self.U[:, c8, 0:nr * stride].rearrange("p (r c) -> p r c", c=stride)[:, :, 15:15 + rl],
                        psA[:, 0:nt].rearrange("p (r c) -> p r c", c=rl),
                        self.t1[:, 0:nt].rearrange("p (r c) -> p r c", c=rl), ALU.mult)
        def conv_pe(c8):
            dg = self.dg31[c8 % 2]
            for o in range(31):
                wc = cols[:, cb + O_DW + c8 * 31 + o:cb + O_DW + c8 * 31 + o + 1]
                if o % 2 == 0:
                    self.ts("dve", dg[:, o, :], self.ident_bf, wc, ALU.mult)
                else:
                    self.act(dg[:, o, :], self.ident_bf, AF.Copy, scale=wc)
            self.pump(4)
            for g in range(nr // rpg):
                ps = self.bank()
                b0 = g * gw
                for ti in range(31):
                    o = ti - 15
                    self.mm(ps[:, 15:gw - 1], dg[:, ti, :], self.U[:, c8, b0 + 15 + o:b0 + gw - 1 + o],
                            start=(ti == 0), stop=(ti == 30))
                self.pump(3)
                src = ps[:, 0:gw].rearrange("p (r c) -> p r c", c=stride)[:, :, 15:15 + rl]
                n0, n1 = g * rpg * rl, (g + 1) * rpg * rl
                self.copy("dve", self.Vf[:, c8, n0:n1].rearrange("p (r c) -> p r c", c=rl), src)
                self.act(self.sq[:, c8, n0:n1].rearrange("p (r c) -> p r c", c=rl), src, AF.Square)
                self.act(self.Vb[:, c8, n0:n1].rearrange("p (r c) -> p r c", c=rl), src, AF.Copy)

        def conv_dve(c8):
            acc = self.cacc
            tot = nr * stride
            L = tot - 16
            order = list(range(1, 31)) + [0]
            for n, ti in enumerate(order):
                yield
                o = ti - 15
                wc = cols[:, cb + O_DW + c8 * 31 + ti:cb + O_DW + c8 * 31 + ti + 1]
                if n == 0:
                    self.ts("dve", acc[:, 0:L], self.U[:, c8, 15 + o:15 + o + L], wc, ALU.mult)
                elif n < 30:
                    self.stt(acc[:, 0:L], self.U[:, c8, 15 + o:15 + o + L], wc, acc[:, 0:L], ALU.mult, ALU.add)
                else:
                    self.stt(self.Vf[:, c8, 0:nt].rearrange("p (r c) -> p r c", c=rl),
                             self.U[:, c8, 0:tot].rearrange("p (r c) -> p r c", c=stride)[:, :, 0:rl], wc,
                             acc[:, 0:tot].rearrange("p (r c) -> p r c", c=stride)[:, :, 0:rl], ALU.mult, ALU.add)
            self.act(self.sq[:, c8, 0:nt], self.Vf[:, c8, 0:nt], AF.Square)
            self.act(self.Vb[:, c8, 0:nt], self.Vf[:, c8, 0:nt], AF.Copy)

        pe_ch, dve_ch = [0, 1, 2, 3, 4], [5, 6, 7]
        jobs = [conv_dve(c) for c in dve_ch]

        def pump(n):
            while n > 0 and jobs:
                try:
                    next(jobs[0])
                    n -= 1
                except StopIteration:
                    jobs.pop(0)

        self.pump = pump
        for i, c8 in enumerate(pe_ch):
            conv_pe(c8)
        for half in range(2):
            W8 = self.load_w(win[:, C_BM + half * 512:C_BM + (half + 1) * 512])
            W7 = self.load_w(win[:, C_BC + half * 512:C_BC + (half + 1) * 512])
            for j in range(4):
                c8 = half * 4 + j
                ps8 = self.bank()
                ps7 = self.bank()
                for kc in range(8):
                    self.mm(ps8[:, 0:nt], W8[:, kc, j * 128:(j + 1) * 128], hTb[:, kc, :], start=(kc == 0), stop=(kc == 7))
                for kc in range(8):
                    self.mm(ps7[:, 0:nt], W7[:, kc, j * 128:(j + 1) * 128], hTb[:, kc, :], start=(kc == 0), stop=(kc == 7))
                self.act(self.S8[:, c8, 0:nt], ps8[:, 0:nt], AF.Sigmoid)
                self.act(self.S7[:, c8, 0:nt], ps7[:, 0:nt], AF.Sigmoid)
                self.pump(6)
        self.pump(10 ** 6)
        psM = self.bank()
        psQ = self.bank()
        for c8 in range(8):
            self.mm(psM[:, 0:nt], self.ones_bf, self.Vb[:, c8, 0:nt], start=(c8 == 0), stop=(c8 == 7))
        for c8 in range(8):
            self.mm(psQ[:, 0:nt], self.ones_bf, self.sq[:, c8, 0:nt], start=(c8 == 0), stop=(c8 == 7))
        mean, msq, var = self.t1, self.t2, self.t3
        self.act(mean[:, 0:nt], psM[:, 0:nt], AF.Copy, scale=1.0 / D)
        self.act(var[:, 0:nt], psQ[:, 0:nt], AF.Copy, scale=1.0 / D)
        self.dma("sp", self.HM[:, :, 0:nt], self.hmscr[:, t0:t1].rearrange("(ec p) t -> p ec t", p=128))
        self.tt("dve", msq[:, 0:nt], mean[:, 0:nt], mean[:, 0:nt], ALU.mult)
        self.tt("dve", var[:, 0:nt], var[:, 0:nt], msq[:, 0:nt], ALU.subtract)
        self.ts("dve", var[:, 0:nt], var[:, 0:nt], EPS, ALU.add)
        self.act(msq[:, 0:nt], var[:, 0:nt], AF.Sqrt)
        self.recip(self.rstd[:, 0:nt], msq[:, 0:nt])
        for c8 in range(8):
            self.tt("dve", self.Vf[:, c8, 0:nt], self.Vf[:, c8, 0:nt], mean[:, 0:nt], ALU.subtract)
            self.tt("dve", self.Vf[:, c8, 0:nt], self.Vf[:, c8, 0:nt], self.rstd[:, 0:nt], ALU.mult)
        for c8 in range(8):
            g = cols[:, cb + O_LNG + c8:cb + O_LNG + c8 + 1]
            b = cols[:, cb + O_LNB + c8:cb + O_LNB + c8 + 1]
            self.act(self.Z[:, c8, 0:nt], self.Vf[:, c8, 0:nt], AF.Sigmoid, bias=b, scale=g)
            self.act(self.Vf[:, c8, 0:nt], self.Vf[:, c8, 0:nt], AF.Identity, bias=b, scale=g)
        for c8 in range(8):
            self.tt("dve", self.Z[:, c8, 0:nt], self.Z[:, c8, 0:nt], self.Vf[:, c8, 0:nt], ALU.mult)
        if bi in (0, 1):
            self.dump("Z%d_%d" % (l, bi), self.Z, [128, 8, 512])
        for half in range(2):
            Wm = [self.load_w(self.w_m_out[l][kh * 1024:(kh + 1) * 1024, half * 512:(half + 1) * 512]) for kh in range(2)]
            for j in range(4):
                c8 = half * 4 + j
                psY = self.bank()
                for ec in range(16):
                    self.mm(psY[:, 0:nt], Wm[ec // 8][:, ec % 8, j * 128:(j + 1) * 128], self.HM[:, ec, 0:nt],
                            start=(ec == 0), stop=(ec == 15))
                self.tt("dve", self.Y[:, c8, 0:nt], psY[:, 0:nt], self.S8[:, c8, 0:nt], ALU.mult)
        for half in range(2):
            Wc = self.load_w(self.w_conv_out[l][:, half * 512:(half + 1) * 512])
            for j in range(4):
                c8 = half * 4 + j
                psY = self.bank()
                for kc in range(8):
                    self.mm(psY[:, 0:nt], Wc[:, kc, j * 128:(j + 1) * 128], self.Z[:, kc, 0:nt], start=(kc == 0), stop=(kc == 7))
                tmp = (self.t2, self.t3)[c8 % 2]
                self.tt("dve", tmp[:, 0:nt], psY[:, 0:nt], self.S7[:, c8, 0:nt], ALU.mult)
                self.tt("dve", self.Ybf[:, c8, 0:nt], tmp[:, 0:nt], self.Y[:, c8, 0:nt], ALU.add)
        if bi in (0, 1):
            self.dump("Y%d_%d" % (l, bi), self.Ybf, [128, 8, 512])
        for half in range(2):
            Wo_ = self.load_w(self.w_out[l][:, half * 512:(half + 1) * 512])
            for j in range(4):
                c8 = half * 4 + j
                psZ = self.bank()
                for kc in range(8):
                    self.mm(psZ[:, 0:nt], Wo_[:, kc, j * 128:(j + 1) * 128], self.Ybf[:, kc, 0:nt], start=(kc == 0), stop=(kc == 7))
                self.stt(xb[:, c8, 0:nt], psZ[:, 0:nt], ga1[:, c8, which:which + 1], xb[:, c8, 0:nt], ALU.mult, ALU.add)
        if bi in (0, 1):
            self.dump("xmid%d_%d" % (l, bi), xb, [128, 8, 512])
        for _ in range(2):
            if self.mod_todo:
                self.mod_piece(*self.mod_todo.pop(0))
        self.rmsnorm_block(xb, nt, self.g2s, sh2, which, self.h2T)
        for u in range(8):
            W1 = self.load_w(self.w_ff1[l][:, u * 512:(u + 1) * 512])
            for j in range(4):
                fc = u * 4 + j
                psH = self.bank()
                for kc in range(8):
                    self.mm(psH[:, 0:nt], W1[:, kc, j * 128:(j + 1) * 128], self.h2T[:, kc, 0:nt], start=(kc == 0), stop=(kc == 7))
                self.act(self.t1[:, 0:nt], psH[:, 0:nt], AF.Relu)
                self.tt("dve", self.Hd[:, fc, 0:nt], psH[:, 0:nt], self.t1[:, 0:nt], ALU.mult)
        for half in range(2):
            psF = [self.bank() for _ in range(4)]
            for kg in range(4):
                W2 = self.load_w(self.w_ff2[l][kg * 1024:(kg + 1) * 1024, half * 512:(half + 1) * 512])
                for j in range(4):
                    for kc in range(8):
                        self.mm(psF[j][:, 0:nt], W2[:, kc, j * 128:(j + 1) * 128], self.Hd[:, kg * 8 + kc, 0:nt],
                                start=(kg == 0 and kc == 0), stop=(kg == 3 and kc == 7))
            for j in range(4):
                c8 = half * 4 + j
                self.stt(xb[:, c8, 0:nt], psF[j][:, 0:nt], ga2[:, c8, which:which + 1], xb[:, c8, 0:nt], ALU.mult, ALU.add)
        if not last:
            self.dma("sp", self.xscr[:, t0:t1].rearrange("(kc p) t -> p kc t", p=128), xb[:, :, 0:nt])
            if self.stop is None:
                modn, g1n, g2n = self.msets[(l + 1) % 2]
                self.rmsnorm_block(xb, nt, g1n, modn[:, 0:8, :], which, self.hT[:, :, t0:t1])
            if bi in (0, 1):
                self.dump("xout%d_%d" % (l, bi), xb, [128, 8, 512])
        else:
            fg = cols[:, O_FG:O_FG + 8].unsqueeze(2)
            ob = self.Vf
            self.rmsnorm_block_f32(xb, nt, fg, ob)
            self.dma("sp", self.outT[:, t0 - CTX:t1 - CTX].rearrange("(kc p) t -> p kc t", p=128), ob[:, :, 0:nt], out_dma=True)

    def rmsnorm_block_f32(self, xb, nt, g, out):
        for kc in range(8):
            self.act(self.sq[:, kc, 0:nt], xb[:, kc, 0:nt], AF.Square)
        ps = self.bank()
        for kc in range(8):
            self.mm(ps[:, 0:nt], self.ones_bf, self.sq[:, kc, 0:nt], start=(kc == 0), stop=(kc == 7))
        self.ts("dve", self.t1[:, 0:nt], ps[:, 0:nt], 1.0 / D, ALU.mult, EPS, ALU.add)
        self.act(self.t2[:, 0:nt], self.t1[:, 0:nt], AF.Sqrt)
        self.recip(self.rstd[:, 0:nt], self.t2[:, 0:nt])
        for kc in range(8):
            self.stt(out[:, kc, 0:nt], xb[:, kc, 0:nt], g[:, kc, 0:1], self.rstd[:, 0:nt], ALU.mult, ALU.mult)


def _colz(v):
    v = np.asarray(v, np.float32)
    return np.ascontiguousarray(v.reshape(-1, 128).T)


def prep_inputs(x, c, ctx, c_ctx, w_mod, b_mod, norm1_g, w_in, mlstm_gate_b, qk_conv_w, conv_dw_w,
                conv_ln_g, conv_ln_b, w_conv_out, mlstm_norm_g, w_mlstm_out, w_out, norm2_g,
                w_ff1, w_ff2, final_g):
    f = lambda a: np.ascontiguousarray(np.asarray(a, np.float32))
    x, c, ctx, c_ctx = f(x), f(c), f(ctx), f(c_ctx)
    consts = np.zeros((128, 640), np.float32)
    idx = np.arange(128)
    consts[idx, idx] = 1.0
    consts[idx, 128 + 127 - idx] = 1.0
    consts[:, 256:384] = (idx[None, :] >= idx[:, None])
    consts[:, 384:512] = (idx[None, :] <= idx[:, None])
    consts[:, 512:640] = 1.0
    shared = dict(consts=consts, w_mod=f(w_mod), w_in=f(w_in), gate_b=f(mlstm_gate_b), mng=f(mlstm_norm_g),
                  w_conv_out=f(w_conv_out), w_m_out=f(w_mlstm_out), w_out=f(w_out), w_ff1=f(w_ff1), w_ff2=f(w_ff2))
    base = np.zeros((128, NCOL), np.float32)
    for l in range(DEPTH):
        cb = l * LC
        base[:, cb + O_BMOD:cb + O_BMOD + 48] = _colz(b_mod[l])
        base[:, cb + O_N1:cb + O_N1 + 8] = _colz(norm1_g[l])
        base[:, cb + O_N2:cb + O_N2 + 8] = _colz(norm2_g[l])
        base[:, cb + O_LNG:cb + O_LNG + 8] = _colz(conv_ln_g[l])
        base[:, cb + O_LNB:cb + O_LNB + 8] = _colz(conv_ln_b[l])
        dw = np.asarray(conv_dw_w[l], np.float32)
        base[:, cb + O_DW:cb + O_DW + 248] = dw.T.reshape(8, 128, 31).transpose(1, 0, 2).reshape(128, 248)
        qk = np.asarray(qk_conv_w[l], np.float32)
        base[:, cb + O_QK:cb + O_QK + 160] = qk.T.reshape(32, 128, 5).transpose(1, 0, 2).reshape(128, 160)
        base[:, cb + O_MNG:cb + O_MNG + 16] = _colz(mlstm_norm_g[l])
    base[:, O_FG:O_FG + 8] = _colz(final_g)
    maps = []
    for b in range(x.shape[0]):
        cols = base.copy()
        cc = np.stack([_colz(c[b]), _colz(c_ctx)], axis=2)
        cols[:, O_C:O_C + 16] = cc.reshape(128, 16)
        xin = np.ascontiguousarray(np.concatenate([ctx[b], x[b]], axis=0).T)
        m = dict(shared)
        m["cols"] = cols
        m["xin"] = xin
        maps.append(m)
    return maps


_NC_CACHE = {}


def kernel(**inputs):
    maps = prep_inputs(**inputs)
    if "nc" not in _NC_CACHE:
        _NC_CACHE["nc"] = Builder().build()
    nc = _NC_CACHE["nc"]
    res = run_bass_kernel_spmd(nc, maps, core_ids=list(range(8)))
    out = np.stack([np.ascontiguousarray(res.results[b]["outT"].T) for b in range(8)], axis=0)
    return out.astype(np.float32)
```
